# Optimizing a Trainium2 kernel written in Bass

```python
import jax
import jax.numpy as jnp
from jax import lax
import numpy as np

D_MODEL = 2048
BATCH = 4
SEQ = 4096
DEPTH = 4

MIX_WIDTH = D_MODEL
EPS = 1e-6
GLA_HEADS = 4
GLA_DV = (MIX_WIDTH // 4) // GLA_HEADS
GLA_DK = GLA_DV // 2
GLA_RANK = 16
GLA_NORMALIZER = 16.0
GLA_CHUNK = 64
SWA_HEADS = 16
SWA_KV_HEADS = 2
SWA_HD = (MIX_WIDTH // 2) // SWA_HEADS
WINDOW = 128
ROPE_THETA = 500000.0
ROPE_DIMS = SWA_HD // 4
GDN_HEADS = 4
GDN_DK = (MIX_WIDTH // 4) // GDN_HEADS
GDN_DV = GDN_DK
CONV_WIDTH = 4
GDN_CHUNK = 64
GDN_CONV_CH = GDN_HEADS * (2 * GDN_DK + GDN_DV)
D_FF = ((8 * D_MODEL + 3 * 256 - 1) // (3 * 256)) * 256

GLA_SIZES = (GLA_HEADS * GLA_DK, GLA_HEADS * GLA_DK, GLA_HEADS * GLA_DV, GLA_HEADS * GLA_DV, GLA_RANK)
SWA_SIZES = (SWA_HEADS * SWA_HD, SWA_KV_HEADS * SWA_HD, SWA_KV_HEADS * SWA_HD)
GDN_SIZES = (GDN_HEADS * GDN_DK, GDN_HEADS * GDN_DK, GDN_HEADS * GDN_DV, GDN_HEADS * GDN_DV, GDN_HEADS, GDN_HEADS)
IN_SIZES = GLA_SIZES + SWA_SIZES + GDN_SIZES
IN_WIDTH = sum(IN_SIZES)

kernel_name = "hybrid_gla_swa_gdn_parallel_heads"


def _split_columns(t, sizes):
    idx, acc = [], 0
    for s in sizes[:-1]:
        acc += s
        idx.append(acc)
    return jnp.split(t, idx, axis=-1)


def rms_norm(x, gain):
    xf = x.astype(jnp.float32)
    y = xf * lax.rsqrt(jnp.mean(xf * xf, axis=-1, keepdims=True) + EPS)
    return (y * gain.astype(jnp.float32)).astype(x.dtype)


def l2_normalize(t):
    return t * lax.rsqrt(jnp.sum(t * t, axis=-1, keepdims=True) + EPS)


def partial_rotary(x, positions):
    half = ROPE_DIMS // 2
    inv_freq = ROPE_THETA ** (-jnp.arange(half, dtype=jnp.float32) / half)
    ang = positions.astype(jnp.float32)[..., None] * inv_freq
    cos = jnp.cos(ang)[:, :, None, :]
    sin = jnp.sin(ang)[:, :, None, :]
    xr = x[..., :ROPE_DIMS].astype(jnp.float32)
    x1, x2 = xr[..., :half], xr[..., half:]
    rot = jnp.concatenate([x1 * cos - x2 * sin, x2 * cos + x1 * sin], axis=-1)
    return jnp.concatenate([rot.astype(x.dtype), x[..., ROPE_DIMS:]], axis=-1)


def _to_chunks(t, chunk):
    B, T, H, d = t.shape
    return t.reshape(B, T // chunk, chunk, H, d).transpose(1, 0, 3, 2, 4)


def _from_chunks(t):
    N, B, H, C, d = t.shape
    return t.transpose(1, 0, 3, 2, 4).reshape(B, N * C, H, d)


def gla_chunked(q, k, v, log_a):
    B, T, H, dk = q.shape
    dv = v.shape[-1]
    C = GLA_CHUNK
    qc = _to_chunks(q * dk ** -0.5, C)
    kc = _to_chunks(k, C)
    vc = _to_chunks(v, C)
    bc = jnp.cumsum(_to_chunks(log_a, C), axis=-2)
    causal = jnp.tril(jnp.ones((C, C), dtype=bool))

    def step(S, inp):
        qi, ki, vi, bi = inp
        diff = bi[..., :, None, :] - bi[..., None, :, :]
        decay = jnp.exp(jnp.where(causal[..., None], diff, -jnp.inf))
        A = jnp.einsum('bhtd,bhsd,bhtsd->bhts', qi, ki, decay)
        o = jnp.einsum('bhts,bhsv->bhtv', A, vi) + jnp.einsum('bhtd,bhdv->bhtv', qi * jnp.exp(bi), S)
        b_last = bi[..., -1:, :]
        S = S * jnp.exp(b_last)[..., 0, :, None] + jnp.einsum(
            'bhsd,bhsv->bhdv', ki * jnp.exp(b_last - bi), vi)
        return S, o

    S0 = jnp.zeros((B, H, dk, dv), jnp.float32)
    _, o = lax.scan(step, S0, (qc, kc, vc, bc))
    return _from_chunks(o)


def gated_delta_chunked(q, k, v, g, beta):
    B, T, H, dk = q.shape
    dv = v.shape[-1]
    C = GDN_CHUNK
    N = T // C
    qc = _to_chunks(q * dk ** -0.5, C)
    kc = _to_chunks(k, C)
    vc = _to_chunks(v, C)
    gc = jnp.cumsum(g.reshape(B, N, C, H).transpose(1, 0, 3, 2), axis=-1)
    bc = beta.reshape(B, N, C, H).transpose(1, 0, 3, 2)
    incl = jnp.tril(jnp.ones((C, C), dtype=bool))
    strict = jnp.tril(jnp.ones((C, C), dtype=bool), k=-1)
    decay = jnp.exp(jnp.where(incl, gc[..., :, None] - gc[..., None, :], -jnp.inf))
    k_beta = kc * bc[..., None]
    L = jnp.where(strict, jnp.einsum('nbhtd,nbhsd->nbhts', k_beta, kc) * decay, 0.0)
    eye = jnp.eye(C, dtype=L.dtype)
    t_inv = lax.linalg.triangular_solve(L + eye, jnp.broadcast_to(eye, L.shape),
                                        left_side=True, lower=True, unit_diagonal=True)
    u = jnp.einsum('nbhts,nbhsv->nbhtv', t_inv, vc * bc[..., None])
    w = jnp.einsum('nbhts,nbhsd->nbhtd', t_inv, k_beta * jnp.exp(gc)[..., None])
    attn = jnp.where(incl, jnp.einsum('nbhtd,nbhsd->nbhts', qc, kc) * decay, 0.0)

    def step(S, inp):
        qi, ki, ui, wi, gi, ai = inp
        v_new = ui - jnp.einsum('bhtd,bhdv->bhtv', wi, S)
        o = jnp.einsum('bhtd,bhdv->bhtv', qi * jnp.exp(gi)[..., None], S) + jnp.einsum(
            'bhts,bhsv->bhtv', ai, v_new)
        g_last = gi[..., -1:]
        S = S * jnp.exp(g_last)[..., None] + jnp.einsum(
            'bhsd,bhsv->bhdv', ki * jnp.exp(g_last - gi)[..., None], v_new)
        return S, o

    S0 = jnp.zeros((B, H, dk, dv), jnp.float32)
    _, o = lax.scan(step, S0, (qc, kc, u, w, gc, attn))
    return _from_chunks(o)


def sliding_window_attention(q, k, v, sinks):
    B, T, _, hd = q.shape
    W = WINDOW
    nb = T // W
    G = SWA_HEADS // SWA_KV_HEADS
    qb = q.reshape(B, nb, W, SWA_KV_HEADS, G, hd)
    kb = k.reshape(B, nb, W, SWA_KV_HEADS, hd)
    vb = v.reshape(B, nb, W, SWA_KV_HEADS, hd)
    zeros = jnp.zeros_like(kb[:, :1])
    kk = jnp.concatenate([jnp.concatenate([zeros, kb[:, :-1]], axis=1), kb], axis=2)
    vv = jnp.concatenate([jnp.concatenate([zeros, vb[:, :-1]], axis=1), vb], axis=2)
    s = jnp.einsum('bnqhgd,bnkhd->bnhgqk', qb, kk,
                   preferred_element_type=jnp.float32) * (hd ** -0.5)
    q_loc = jnp.arange(W)[:, None] + W
    k_loc = jnp.arange(2 * W)[None, :]
    dist = q_loc - k_loc
    band = (dist >= 0) & (dist < WINDOW)
    has_prev = (jnp.arange(nb)[:, None, None] > 0) | (k_loc >= W)[None]
    valid = band[None] & has_prev
    s = jnp.where(valid[None, :, None, None], s, -jnp.inf)
    sink = sinks.astype(jnp.float32).reshape(SWA_KV_HEADS, G)[None, None, :, :, None, None]
    m = jnp.maximum(jnp.max(s, axis=-1, keepdims=True), sink)
    p = jnp.exp(s - m)
    probs = p / (jnp.sum(p, axis=-1, keepdims=True) + jnp.exp(sink - m))
    o = jnp.einsum('bnhgqk,bnkhd->bnqhgd', probs.astype(v.dtype), vv)
    return o.reshape(B, T, SWA_HEADS * hd)


def causal_depthwise_conv(t, w):
    ch = t.shape[-1]
    return lax.conv_general_dilated(
        t, w.astype(t.dtype)[:, None, :], window_strides=(1,), padding=((CONV_WIDTH - 1, 0),),
        dimension_numbers=('NWC', 'WIO', 'NWC'), feature_group_count=ch)


def hybrid_token_mixer(h, positions, w_in, gla_w_gk, gla_b_gk, gla_norm_gain, swa_sinks,
                       gdn_conv_w, gdn_a_log, gdn_dt_bias, gdn_norm_gain, w_out):
    B, T, _ = h.shape
    f32 = jnp.float32
    proj = h @ w_in
    (aq, ak, av, ag, alr, bq, bk, bv, cq, ck, cv, cz, cb, ca) = _split_columns(proj, IN_SIZES)

    log_a = jax.nn.log_sigmoid((alr @ gla_w_gk + gla_b_gk).astype(f32)) / GLA_NORMALIZER
    o_a = gla_chunked(aq.astype(f32).reshape(B, T, GLA_HEADS, GLA_DK),
                      ak.astype(f32).reshape(B, T, GLA_HEADS, GLA_DK),
                      av.astype(f32).reshape(B, T, GLA_HEADS, GLA_DV),
                      log_a.reshape(B, T, GLA_HEADS, GLA_DK))
    o_a = rms_norm(o_a, gla_norm_gain) * jax.nn.silu(ag.astype(f32).reshape(B, T, GLA_HEADS, GLA_DV))

    qs = partial_rotary(bq.reshape(B, T, SWA_HEADS, SWA_HD), positions)
    ks = partial_rotary(bk.reshape(B, T, SWA_KV_HEADS, SWA_HD), positions)
    o_b = sliding_window_attention(qs, ks, bv.reshape(B, T, SWA_KV_HEADS, SWA_HD), swa_sinks)

    qkv = jax.nn.silu(causal_depthwise_conv(jnp.concatenate([cq, ck, cv], axis=-1), gdn_conv_w))
    cq, ck, cv = _split_columns(qkv.astype(f32), (GDN_HEADS * GDN_DK, GDN_HEADS * GDN_DK, GDN_HEADS * GDN_DV))
    beta = jax.nn.sigmoid(cb.astype(f32))
    g = -jnp.exp(gdn_a_log.astype(f32)) * jax.nn.softplus(ca.astype(f32) + gdn_dt_bias.astype(f32))
    o_c = gated_delta_chunked(l2_normalize(cq.reshape(B, T, GDN_HEADS, GDN_DK)),
                              l2_normalize(ck.reshape(B, T, GDN_HEADS, GDN_DK)),
                              cv.reshape(B, T, GDN_HEADS, GDN_DV), g, beta)
    o_c = rms_norm(o_c, gdn_norm_gain) * jax.nn.silu(cz.astype(f32).reshape(B, T, GDN_HEADS, GDN_DV))

    mix = jnp.concatenate([o_a.reshape(B, T, -1).astype(h.dtype), o_b.astype(h.dtype),
                           o_c.reshape(B, T, -1).astype(h.dtype)], axis=-1)
    return mix @ w_out


def swiglu_ffn(h, w_gate, w_up, w_down):
    return (jax.nn.silu(h @ w_gate) * (h @ w_up)) @ w_down


def setup_inputs(seed: int = 0) -> dict:
    key = jax.random.key(seed)
    ks = jax.random.split(key, 24)
    f32 = jnp.float32
    D = D_MODEL

    def nrm(k, shape, fan_in, scale=1.0):
        return jax.random.normal(k, shape, f32) * (scale * fan_in ** -0.5)

    def gain(k, shape):
        return 1.0 + 0.02 * jax.random.normal(k, shape, f32)

    x = jax.random.normal(ks[0], (BATCH, SEQ, D), f32)
    c = jax.random.normal(ks[1], (BATCH, D), f32)
    positions = (jax.random.randint(ks[2], (BATCH, 1), 0, 1024, dtype=jnp.int32)
                 + jnp.arange(SEQ, dtype=jnp.int32)[None, :])
    w_mod = nrm(ks[3], (DEPTH, D, 6 * D), D, 0.5)
    b_mod = 0.02 * jax.random.normal(ks[4], (DEPTH, 6 * D), f32)
    norm1_gain = gain(ks[5], (DEPTH, D))
    norm2_gain = gain(ks[6], (DEPTH, D))
    w_in = nrm(ks[7], (DEPTH, D, IN_WIDTH), D)
    gla_w_gk = nrm(ks[8], (DEPTH, GLA_RANK, GLA_HEADS * GLA_DK), GLA_RANK)
    gla_b_gk = 0.1 * jax.random.normal(ks[9], (DEPTH, GLA_HEADS * GLA_DK), f32)
    gla_norm_gain = gain(ks[10], (DEPTH, GLA_DV))
    swa_sinks = jax.random.normal(ks[11], (DEPTH, SWA_HEADS), f32)
    gdn_conv_w = nrm(ks[12], (DEPTH, CONV_WIDTH, GDN_CONV_CH), CONV_WIDTH)
    gdn_a_log = jnp.log(jax.random.uniform(ks[13], (DEPTH, GDN_HEADS), f32, 1.0, 16.0))
    dt = jnp.exp(jax.random.uniform(ks[14], (DEPTH, GDN_HEADS), f32, float(np.log(1e-3)), float(np.log(1e-1))))
    gdn_dt_bias = dt + jnp.log(-jnp.expm1(-dt))
    gdn_norm_gain = gain(ks[15], (DEPTH, GDN_DV))
    w_out = nrm(ks[16], (DEPTH, MIX_WIDTH, D), MIX_WIDTH)
    ffn_w_gate = nrm(ks[17], (DEPTH, D, D_FF), D)
    ffn_w_up = nrm(ks[18], (DEPTH, D, D_FF), D)
    ffn_w_down = nrm(ks[19], (DEPTH, D_FF, D), D_FF)
    final_norm_gain = gain(ks[20], (D,))
    return {"x": x, "c": c, "positions": positions, "w_mod": w_mod, "b_mod": b_mod,
            "norm1_gain": norm1_gain, "norm2_gain": norm2_gain, "w_in": w_in,
            "gla_w_gk": gla_w_gk, "gla_b_gk": gla_b_gk, "gla_norm_gain": gla_norm_gain,
            "swa_sinks": swa_sinks, "gdn_conv_w": gdn_conv_w, "gdn_a_log": gdn_a_log,
            "gdn_dt_bias": gdn_dt_bias, "gdn_norm_gain": gdn_norm_gain, "w_out": w_out,
            "ffn_w_gate": ffn_w_gate, "ffn_w_up": ffn_w_up, "ffn_w_down": ffn_w_down,
            "final_norm_gain": final_norm_gain}


def reference(x, c, positions, w_mod, b_mod, norm1_gain, norm2_gain, w_in, gla_w_gk, gla_b_gk,
              gla_norm_gain, swa_sinks, gdn_conv_w, gdn_a_log, gdn_dt_bias, gdn_norm_gain, w_out,
              ffn_w_gate, ffn_w_up, ffn_w_down, final_norm_gain):
    c_act = jax.nn.silu(c)
    for l in range(DEPTH):
        mod = c_act @ w_mod[l] + b_mod[l]
        shift1, scale1, gate1, shift2, scale2, gate2 = jnp.split(mod[:, None, :], 6, axis=-1)
        h = rms_norm(x, norm1_gain[l]) * (1.0 + scale1) + shift1
        x = x + gate1 * hybrid_token_mixer(h, positions, w_in[l], gla_w_gk[l], gla_b_gk[l],
                                           gla_norm_gain[l], swa_sinks[l], gdn_conv_w[l],
                                           gdn_a_log[l], gdn_dt_bias[l], gdn_norm_gain[l], w_out[l])
        h = rms_norm(x, norm2_gain[l]) * (1.0 + scale2) + shift2
        x = x + gate2 * swiglu_ffn(h, ffn_w_gate[l], ffn_w_up[l], ffn_w_down[l])
    return rms_norm(x, final_norm_gain)
```

```python
import numpy as np
import concourse.bass as bass
import concourse.mybir as mybir
from concourse.bass_utils import run_bass_kernel_spmd

F32 = mybir.dt.float32
BF16 = mybir.dt.bfloat16
I32 = mybir.dt.int32
U8 = mybir.dt.uint8
AF = mybir.ActivationFunctionType
ALU = mybir.AluOpType
AX = mybir.AxisListType
ISZ = {F32: 4, BF16: 2, I32: 4, U8: 1}

D = 2048
KC = 16
DFF = 5632
FC = 44
INW = 4888
EPS = 1e-6
TT = 512
NBLK = 4
SCHEDULE = True
EMBED_WAIT = True
CRITPATH = True


class Op:
    __slots__ = ("eng", "fn", "deps", "odeps", "sig", "val", "grp", "isdma", "q", "cost", "idx", "fin", "nsucc", "succ", "npend", "est", "bl")

    def __init__(self, eng, fn, isdma=False, grp=None, q=None):
        self.eng = eng
        self.fn = fn
        self.deps = []
        self.odeps = []
        self.cost = 100.0
        self.sig = False
        self.val = 0
        self.grp = grp
        self.isdma = isdma
        self.q = q


class DmaGroup:
    def __init__(self, name, wait_total=False):
        self.name = name
        self.count = 0
        self.sem = None
        self.wait_total = wait_total
        self.cur = []

    def close(self):
        for op in self.cur:
            op.val = self.count
        self.cur = []


GRAN = 1024


class Sched:
    ENGS = ("pe", "act", "dve", "pool", "sp")

    def __init__(self, nc):
        self.nc = nc
        self.streams = {e: [] for e in self.ENGS}
        self.acc = {}
        self.keys_w = {}
        self.keys_r = {}
        self.groups = []
        self.nops = 0

    def group(self, name, wait_total=False):
        g = DmaGroup(name, wait_total)
        self.groups.append(g)
        return g

    @staticmethod
    def region(ap):
        sp = str(ap.space)
        isz = ISZ[ap.dtype]
        pstride, pcnt = ap.ap[0]
        off = ap.offset
        if pstride:
            p0 = off // pstride
            f0 = off % pstride
        else:
            p0 = 0
            f0 = off
        ext = 1
        for s, c in ap.ap[1:]:
            ext += abs(s) * (c - 1)
        return (sp + ap.tensor.name, p0, p0 + pcnt, f0 * isz, (f0 + ext) * isz)

    def _dep(self, op, other, a_write, b_write):
        if other is op:
            return
        if other.isdma or op.isdma:
            if other.isdma and op.isdma and other.grp is op.grp and other.q == op.q:
                op.odeps.append(other)
                return
            op.deps.append(other)
            return
        if other.eng == op.eng:
            if op.eng == "pe":
                op.odeps.append(other)
                return
            op.deps.append(other)
            return
        op.deps.append(other)

    def _access(self, op, ap, is_write):
        sp, p0, p1, lo, hi = self.region(ap)
        psum = ap.tensor.name == "ps"
        if psum:
            p0, p1 = 0, 128
            lo = (lo // 2048) * 2048
            hi = ((hi + 2047) // 2048) * 2048
        g0 = lo // GRAN
        g1 = (hi - 1) // GRAN
        for g in range(g0, g1 + 1):
            key = (sp, g)
            lst = self.acc.get(key)
            if lst is None:
                lst = []
                self.acc[key] = lst
            glo = max(lo, g * GRAN)
            ghi = min(hi, (g + 1) * GRAN)
            keep = []
            for e in lst:
                ep0, ep1, elo, ehi, eop, ew = e
                ov = ep0 < p1 and p0 < ep1 and elo < ghi and glo < ehi
                conflict = ov and (is_write or ew or (psum and eop.eng != op.eng))
                if conflict:
                    self._dep(op, eop, is_write, ew)
                if ov and ep0 >= p0 and ep1 <= p1 and elo >= glo and ehi <= ghi:
                    if is_write:
                        continue
                keep.append(e)
            keep.append((p0, p1, glo, ghi, op, is_write))
            self.acc[key] = keep

    def add(self, eng, fn, r=(), w=(), kr=(), kw=(), dma=None, cost=None):
        op = Op(eng, fn, isdma=dma is not None, grp=dma, q=eng)
        op.idx = self.nops
        if cost is None:
            n = 1
            aps = list(w) + list(r)
            if aps:
                for d_ in aps[0].shape[1:]:
                    n *= d_
            cost = 120.0 + n / 0.96 * (1.6 if eng == "pool" else 1.0)
        op.cost = cost
        for ap in r:
            self._access(op, ap, False)
        for ap in w:
            self._access(op, ap, True)
        for k in kr:
            o = self.keys_w.get(k)
            if o is not None:
                self._dep(op, o, False, True)
            self.keys_r.setdefault(k, []).append(op)
        for k in kw:
            o = self.keys_w.get(k)
            if o is not None:
                self._dep(op, o, True, True)
            for o in self.keys_r.get(k, []):
                self._dep(op, o, True, False)
            self.keys_w[k] = op
            self.keys_r[k] = []
        if dma is not None:
            dma.count += 16
            op.val = dma.count
            dma.cur.append(op)
        for d in op.deps:
            d.sig = True
        self.streams[eng].append(op)
        self.nops += 1
        return op

    def schedule(self):
        import heapq
        allops = []
        for e in self.ENGS:
            allops.extend(self.streams[e])
        allops.sort(key=lambda o: o.idx)
        for o in allops:
            o.succ = []
            o.fin = None
        for o in allops:
            ds = set(id(d) for d in o.deps) | set(id(d) for d in o.odeps)
            o.npend = 0
            seen = set()
            for d in list(o.deps) + list(o.odeps):
                if id(d) in seen:
                    continue
                seen.add(id(d))
                d.succ.append(o)
                o.npend += 1
        LAT = 300.0
        for o in reversed(allops):
            b = 0.0
            for s_ in o.succ:
                if s_.bl + LAT > b:
                    b = s_.bl + LAT
            o.bl = b + (o.cost if not o.isdma else o.cost + 120.0)
        free = {e: 0.0 for e in self.ENGS}
        waiting = {e: [] for e in self.ENGS}
        avail = {e: [] for e in self.ENGS}
        order = {e: [] for e in self.ENGS}

        def push(o):
            est = 0.0
            for d in o.deps:
                f = d.fin + LAT
                if f > est:
                    est = f
            for d in o.odeps:
                f = d.fin - d.cost if not d.isdma else d.est
                if f > est:
                    est = f
            o.est = est
            heapq.heappush(waiting[o.eng], (est, o.idx, o))
        for o in allops:
            if o.npend == 0:
                push(o)
        remaining = len(allops)
        while remaining:
            best = None
            for e in self.ENGS:
                w_, a_ = waiting[e], avail[e]
                while w_ and w_[0][0] <= free[e]:
                    est, idx, o = heapq.heappop(w_)
                    heapq.heappush(a_, (-o.bl if CRITPATH else idx, idx, o))
                if a_:
                    cand = (free[e], a_[0][1], e, True)
                elif w_:
                    cand = (w_[0][0], w_[0][1], e, False)
                else:
                    continue
                if best is None or cand < best:
                    best = cand
            t, idx, e, isav = best
            if isav:
                _, idx, o = heapq.heappop(avail[e])
            else:
                est, idx, o = heapq.heappop(waiting[e])
            start = max(free[e], o.est)
            if o.isdma:
                issue = 1500.0 if e == "pool" else 120.0
                free[e] = start + issue
                o.est = start
                o.fin = start + issue + o.cost
            else:
                free[e] = start + o.cost
                o.fin = free[e]
            order[e].append(o)
            remaining -= 1
            for s_ in o.succ:
                s_.npend -= 1
                if s_.npend == 0:
                    push(s_)
        for e in self.ENGS:
            assert len(order[e]) == len(self.streams[e])
            self.streams[e] = order[e]
        self.makespan = max(free.values())

    def emit(self, final_ops):
        nc = self.nc
        for e in self.ENGS:
            cnt = 0
            for op in self.streams[e]:
                if not op.isdma and op.sig:
                    cnt += 1
                    op.val = cnt
        import contextlib
        with contextlib.ExitStack() as es:
            esem = {e: es.enter_context(nc.semaphore("c_" + e)) for e in self.ENGS}
            for g in self.groups:
                g.sem = es.enter_context(nc.semaphore("d_" + g.name))
            block = es.enter_context(nc.Block())

            def run(e, eng):
                waited = {}
                for op in self.streams[e]:
                    need = {}
                    for d in op.deps:
                        if d.isdma:
                            sem = d.grp.sem
                            v = d.grp.count if d.grp.wait_total else d.val
                        else:
                            sem = esem[d.eng]
                            v = d.val
                        k = id(sem)
                        if waited.get(k, 0) >= v:
                            continue
                        if k not in need or need[k][1] < v:
                            need[k] = (sem, v)
                    items = list(need.items())
                    embed = None
                    if EMBED_WAIT and items and not op.isdma:
                        embed = items.pop()
                    for k, (sem, v) in items:
                        eng.wait_ge(sem, v)
                        waited[k] = v
                    ins = op.fn(eng)
                    if embed is not None:
                        k, (sem, v) = embed
                        ins._wait_ge(sem, v)
                        waited[k] = v
                    if op.isdma:
                        ins.then_inc(op.grp.sem, 16)
                    elif op.sig:
                        ins.then_inc(esem[e], 1)
                if e == "sp":
                    for d in final_ops:
                        sem = d.grp.sem
                        eng.wait_ge(sem, d.grp.count)

            @block.tensor
            def _(t):
                run("pe", t)

            @block.scalar
            def _(s):
                run("act", s)

            @block.vector
            def _(v):
                run("dve", v)

            @block.gpsimd
            def _(p):
                run("pool", p)

            @block.sync
            def _(s):
                run("sp", s)


class Arena:
    def __init__(self, nc, nbytes):
        self.t = nc.alloc_sbuf_tensor("arena", [128, nbytes], U8)
        self.nbytes = nbytes
        self.top = 0

    def at(self, off, shape, dt):
        n = 1
        for x in shape:
            n *= x
        nb = n * ISZ[dt]
        assert off % 4 == 0 and off + nb <= self.nbytes, (off, nb, self.nbytes)
        ap = self.t[:, off:off + nb].bitcast(dt)
        if len(shape) == 2:
            ap = ap.rearrange("p (a b) -> p a b", a=shape[0])
        elif len(shape) == 3:
            ap = ap.rearrange("p (a b c) -> p a b c", a=shape[0], b=shape[1])
        elif len(shape) == 4:
            ap = ap.rearrange("p (a b c d) -> p a b c d", a=shape[0], b=shape[1], c=shape[2])
        return ap

    def alloc(self, shape, dt):
        n = 1
        for x in shape:
            n *= x
        nb = (n * ISZ[dt] + 63) // 64 * 64
        ap = self.at(self.top, shape, dt)
        self.top += nb
        return ap


C_IDENT, C_TRIN, C_TRII, C_TRIR, C_ONES, C_MN1, C_MN0, C_PROT, C_LVL = 0, 1, 2, 3, 4, 5, 6, 7, 8
NCONST = 15
BIGNEG = -30000.0


def make_consts():
    s = np.arange(128)[:, None]
    t = np.arange(128)[None, :]
    c = np.zeros((128, NCONST, 128), np.float32)
    c[:, C_IDENT] = (s == t)
    c[:, C_TRIN] = np.where(s <= t, -1.0 / 16.0, 0.0)
    c[:, C_TRII] = (s <= t)
    c[:, C_TRIR] = (s > t)
    c[:, C_ONES] = 1.0
    c[:, C_MN1] = np.where(s < t, 0.0, BIGNEG)
    c[:, C_MN0] = np.where(s <= t, 0.0, BIGNEG)
    pr = np.zeros((128, 128), np.float32)
    for base in (0, 64):
        for d in range(8):
            pr[base + d + 8, base + d] = -1.0
            pr[base + d, base + d + 8] = 1.0
    c[:, C_PROT] = pr
    for j in range(1, 8):
        b = 1 << (j - 1)
        m = ((s // (2 * b)) == (t // (2 * b))) & (((s // b) % 2) == 0) & (((t // b) % 2) == 1)
        c[:, C_LVL + j - 1] = m
    cv = np.zeros((128, 8), np.float32)
    half = 8
    invf = (500000.0 ** (-np.arange(half, dtype=np.float32) / np.float32(half))).astype(np.float32)
    for base in (0, 64):
        for d in range(16):
            cv[base + d, 0] = invf[d % 8]
    return c, cv


def small_layout(L):
    o = {}
    off = 0

    def put(name, n):
        nonlocal off
        o[name] = (off, n)
        off += n
    put("c", 16)
    put("fng", 16)
    put("n1g", L * 16)
    put("n2g", L * 16)
    put("bmod", L * 96)
    put("gagain", L * 128)
    put("gcgain", L * 128)
    put("sinks", L * 16)
    put("convw", L * 48)
    put("alog", L * 4)
    put("dtb", L * 4)
    put("wgk", L * 256)
    put("cvec", 8)
    return o, off


def pack_small(inp, b, L):
    lay, n = small_layout(L)
    a = np.zeros((128, n), np.float32)

    def fm(v, nch):
        return np.ascontiguousarray(v.reshape(nch, 128).T)

    def setv(name, arr):
        o, k = lay[name]
        a[:, o:o + k] = arr.reshape(128, k)
    setv("c", fm(inp["c"][b], 16))
    setv("fng", fm(inp["final_norm_gain"], 16))
    setv("n1g", np.stack([fm(inp["norm1_gain"][l], 16) for l in range(L)], 1))
    setv("n2g", np.stack([fm(inp["norm2_gain"][l], 16) for l in range(L)], 1))
    setv("bmod", np.stack([fm(inp["b_mod"][l], 96) for l in range(L)], 1))
    setv("gagain", np.broadcast_to(inp["gla_norm_gain"][:L].reshape(1, L * 128), (128, L * 128)))
    setv("gcgain", np.broadcast_to(inp["gdn_norm_gain"][:L].reshape(1, L * 128), (128, L * 128)))
    setv("sinks", np.broadcast_to(inp["swa_sinks"][:L].reshape(1, L * 16), (128, L * 16)))
    cw = np.stack([np.stack([fm(inp["gdn_conv_w"][l, j], 12) for j in range(4)], 1) for l in range(L)], 1)
    setv("convw", cw)
    setv("alog", np.broadcast_to(inp["gdn_a_log"][:L].reshape(1, L * 4), (128, L * 4)))
    setv("dtb", np.broadcast_to(inp["gdn_dt_bias"][:L].reshape(1, L * 4), (128, L * 4)))
    wg = np.zeros((128, L, 256), np.float32)
    wg[:16] = np.transpose(inp["gla_w_gk"][:L], (1, 0, 2))
    wg[16] = inp["gla_b_gk"][:L]
    setv("wgk", wg)
    _, cv = make_consts()
    setv("cvec", cv)
    return a


IN_BLOCKS = [
    ("bq0", [(1552, 256)]), ("bq1", [(1808, 256)]), ("bq2", [(2064, 256)]), ("bq3", [(2320, 256)]),
    ("bk", [(2576, 64), (2576, 64), (2640, 64), (2640, 64)]), ("bv", [(2704, 128)]),
    ("aq", [(0, 256)]), ("ak", [(256, 256)]), ("alr", [(1536, 16)]),
    ("av0", [(512, 256)]), ("av1", [(768, 256)]), ("ag0", [(1024, 256)]), ("ag1", [(1280, 256)]),
    ("cq0", [(2832, 256)]), ("cq1", [(3088, 256)]), ("ck0", [(3344, 256)]), ("ck1", [(3600, 256)]),
    ("cv0", [(3856, 256)]), ("cv1", [(4112, 256)]), ("cz0", [(4368, 256)]), ("cz1", [(4624, 256)]),
    ("cbca", [(4880, 8)]),
]


def build(T, L, stop_after=None, dumps=()):
    nc = bass.Bass("TRN2", target_bir_lowering=False)
    S = Sched(nc)
    NT = T // TT
    lay, NS = small_layout(L)
    dumps = set(dumps)

    x_d = nc.dram_tensor("x", [T, D], F32, kind="ExternalInput").ap()
    pos_d = nc.dram_tensor("pos", [128, T], I32, kind="ExternalInput").ap()
    small_d = nc.dram_tensor("small", [128, NS], F32, kind="ExternalInput").ap()
    const_d = nc.dram_tensor("consts", [128, NCONST, 128], F32, kind="ExternalInput").ap()
    wmod_d = nc.dram_tensor("w_mod", [L, D, 6 * D], F32, kind="ExternalInput").ap()
    win_d = nc.dram_tensor("w_in", [L, D, INW], F32, kind="ExternalInput").ap()
    wout_d = nc.dram_tensor("w_out", [L, D, D], F32, kind="ExternalInput").ap()
    wg_d = nc.dram_tensor("w_gate", [L, D, DFF], F32, kind="ExternalInput").ap()
    wu_d = nc.dram_tensor("w_up", [L, D, DFF], F32, kind="ExternalInput").ap()
    wd_d = nc.dram_tensor("w_down", [L, DFF, D], F32, kind="ExternalInput").ap()
    out_d = nc.dram_tensor("out", [T, D], F32, kind="ExternalOutput").ap()

    in_off = {}
    o = 0
    for name, segs in IN_BLOCKS:
        w = sum(s[1] for s in segs)
        in_off[name] = (o, w)
        o += 128 * KC * w
    IN_ELEMS = o
    sc_in = [nc.dram_tensor("sc_in%d" % l, [IN_ELEMS], BF16, kind="Internal").ap() for l in range(L)]
    sc_out = [nc.dram_tensor("sc_out%d" % l, [8, 128, KC, 256], BF16, kind="Internal").ap() for l in range(L)]
    sc_g = [nc.dram_tensor("sc_g%d" % l, [22, 128, KC, 256], BF16, kind="Internal").ap() for l in range(L)]
    sc_u = [nc.dram_tensor("sc_u%d" % l, [22, 128, KC, 256], BF16, kind="Internal").ap() for l in range(L)]
    sc_d = [nc.dram_tensor("sc_d%d" % l, [32, 128, 22, 128], BF16, kind="Internal").ap() for l in range(L)]

    def in_view(l, name):
        o, w = in_off[name]
        return sc_in[l][o:o + 128 * KC * w].rearrange("(p k c) -> p k c", p=128, k=KC), w

    A = Arena(nc, 208000)
    consts = A.alloc([NCONST, 128], F32)
    cb16 = A.alloc([4, 128], BF16)
    small = A.alloc([NS], F32)
    modT = A.alloc([L, 6, 16], F32)
    scalA = A.alloc([L, 2, 16], F32)
    esink = A.alloc([L, 16], F32)
    aexp = A.alloc([L, 4], F32)
    xT = A.alloc([KC, TT], F32)
    hT = A.alloc([KC, TT], BF16)
    NWB = 3
    wbuf = [A.alloc([4096], BF16) for _ in range(NWB)]
    cosT = A.alloc([TT], F32)
    sinT = A.alloc([TT], F32)
    glaS = [A.alloc([4, 128], F32) for _ in range(L)]
    glaSb = [A.alloc([4, 128], BF16) for _ in range(L)]
    gdnS = [A.alloc([4, 128], F32) for _ in range(L)]
    gdnSb = [A.alloc([4, 128], BF16) for _ in range(L)]
    kcar = [A.alloc([2, 128], BF16) for _ in range(L)]
    vcar = [A.alloc([2, 66], BF16) for _ in range(L)]
    ccar = [A.alloc([12, 4], F32) for _ in range(L)]
    RBASE = A.top
    RSIZE = A.nbytes - RBASE

    def sm(name):
        o, n = lay[name]
        return small[:, o:o + n]

    ps = nc.alloc_psum_tensor("ps", [128, 4096], F32)

    def bank(b, n=512):
        return ps[:, b * 512:b * 512 + n]

    C = lambda i: consts[:, i, :]
    identb, onesb, maskCb, maskPb = cb16[:, 0, :], cb16[:, 1, :], cb16[:, 2, :], cb16[:, 3, :]

    def mm(out, lhsT, rhs, start=True, stop=True):
        n = rhs.shape[-1]
        c = max(n, 64) / 2.4 * (4.0 if rhs.dtype == F32 else 1.0) + 6.0
        return S.add("pe", lambda e: e.matmul(out, lhsT, rhs, start=start, stop=stop), r=[lhsT, rhs], w=[out], cost=c)

    def tr(out, in_, ident):
        c = 64.0 * (4.0 if in_.dtype == F32 else 1.0)
        return S.add("pe", lambda e: e.transpose(out, in_, ident), r=[in_, ident], w=[out], cost=c)

    def act(out, in_, func, bias=0.0, scale=1.0, accum=None):
        r = [in_]
        if not isinstance(bias, float):
            r.append(bias)
        if not isinstance(scale, float):
            r.append(scale)
        w = [out] + ([accum] if accum is not None else [])
        if accum is not None:
            return S.add("act", lambda e: e.activation(out, in_, func, bias=bias, scale=scale, accum_out=accum), r=r, w=w)
        n = 1
        for d_ in out.shape[1:]:
            n *= d_
        c = 120.0 + n / 0.96 + (2600.0 if func == AF.Ln else 0.0)
        return S.add("act", lambda e: e.activation(out, in_, func, bias=bias, scale=scale), r=r, w=w, cost=c)

    def tt(eng, out, in0, in1, op):
        return S.add(eng, lambda e: e.tensor_tensor(out, in0, in1, op), r=[in0, in1], w=[out])

    def ts(eng, out, in0, s1, s2, op0, op1=None):
        r = [in0] + [s for s in (s1, s2) if s is not None and not isinstance(s, float)]
        if op1 is None:
            return S.add(eng, lambda e: e.tensor_scalar(out, in0, s1, None, op0), r=r, w=[out])
        return S.add(eng, lambda e: e.tensor_scalar(out, in0, s1, s2, op0, op1), r=r, w=[out])

    def stt(eng, out, in0, sc, in1, op0, op1):
        r = [in0, in1] + ([] if isinstance(sc, float) else [sc])
        return S.add(eng, lambda e: e.scalar_tensor_tensor(out, in0, sc, in1, op0, op1), r=r, w=[out])

    def cp(eng, out, in_):
        if eng == "act":
            return S.add("act", lambda e: e.copy(out, in_), r=[in_], w=[out])
        return S.add(eng, lambda e: e.tensor_copy(out, in_), r=[in_], w=[out])

    def memset(eng, out, v):
        return S.add(eng, lambda e: e.memset(out, v), w=[out])

    def dma(q, out, in_, grp, r=(), w=(), kr=(), kw=()):
        nb = ISZ[out.dtype]
        for d_ in out.shape:
            nb *= d_
        return S.add(q, lambda e: e.dma_start(out=out, in_=in_), r=r, w=w, kr=kr, kw=kw, dma=grp,
                     cost=2000.0 + nb / 150.0)
    dma_real = dma

    dump_ops = []
    g_dump = S.group("dump")

    def dump(name, ap, shape):
        if name not in dumps:
            return
        dd = nc.dram_tensor("dbg_" + name, [ap.shape[0]] + list(shape), ap.dtype, kind="ExternalOutput").ap()
        dump_ops.append(dma("sp", dd, ap, g_dump, r=[ap]))

    g_pro = S.group("pro")
    g_pos = S.group("pos")
    dma("sp", consts, const_d, g_pro, w=[consts])
    dma("sp", small, small_d, g_pro, w=[small])
    g_pro.close()
    for i, ci in enumerate((C_IDENT, C_ONES, C_TRII, C_TRIR)):
        cp("dve", cb16[:, i, :], C(ci))
    for l in range(L):
        memset("pool", glaS[l], 0.0)
        memset("pool", glaSb[l], 0.0)
        memset("pool", gdnS[l], 0.0)
        memset("pool", gdnSb[l], 0.0)
        memset("pool", kcar[l], 0.0)
        memset("pool", vcar[l], 0.0)
        memset("pool", ccar[l], 0.0)
    act(esink, sm("sinks").rearrange("p (l h) -> p l h", l=L), AF.Exp)
    act(aexp, sm("alog").rearrange("p (l h) -> p l h", l=L), AF.Exp)
    ts("dve", aexp, aexp, -1.0, None, ALU.mult)
    cact = A.at(RBASE, [16], F32)
    act(cact, sm("c"), AF.Silu)

    g_cast = {}

    pending_casts = {}

    def cast_layer(l, defer=False):
        lst = []

        def dma(q, out, in_, grp, **kw):
            lst.append(lambda: dma_real(q, out, in_, grp, **kw))
        for nm in ("in", "out", "g", "u", "d"):
            g_cast[(l, nm)] = S.group("cast_%s%d" % (nm, l), wait_total=True)
        gi = g_cast[(l, "in")]
        for name, segs in IN_BLOCKS:
            v, w = in_view(l, name)
            c0 = 0
            for (sc, sw) in segs:
                for k0 in range(0, KC, 4):
                    src = win_d[l, k0 * 128:(k0 + 4) * 128, sc:sc + sw].rearrange("(k p) c -> p k c", p=128)
                    dma("pool", v[:, k0:k0 + 4, c0:c0 + sw], src, gi, kw=[("sc_in", l)])
                c0 += sw
        for (nm, scr, wd, nb) in (("out", sc_out, wout_d, 8), ("g", sc_g, wg_d, 22), ("u", sc_u, wu_d, 22)):
            gg = g_cast[(l, nm)]
            for b in range(nb):
                for k0 in range(0, KC, 4):
                    src = wd[l, k0 * 128:(k0 + 4) * 128, b * 256:(b + 1) * 256].rearrange("(k p) c -> p k c", p=128)
                    dma("pool", scr[l][b, :, k0:k0 + 4, :], src, gg, kw=[("sc_" + nm, l)])
        gg = g_cast[(l, "d")]
        for m in range(16):
            for kh in range(2):
                for k0 in (0, 11):
                    kk = kh * 22 + k0
                    src = wd_d[l, kk * 128:(kk + 11) * 128, m * 128:(m + 1) * 128].rearrange("(k p) c -> p k c", p=128)
                    dma("pool", sc_d[l][m * 2 + kh, :, k0:k0 + 11, :], src, gg, kw=[("sc_d", l)])
        if defer:
            pending_casts[l] = lst
        else:
            for f in lst:
                f()

    def emit_casts(l, frac_idx, nfrac):
        lst = pending_casts.get(l)
        if not lst:
            return
        n = len(lst)
        for f in lst[n * frac_idx // nfrac:n * (frac_idx + 1) // nfrac]:
            f()

    cast_layer(0)

    wm_stage = [A.at(RBASE + 64 + i * 32768, [KC, 512], F32) for i in range(2)]
    macc = A.at(RBASE + 64 + 65536, [512], F32)
    g_wm = [S.group("wm%d" % i) for i in range(2)]
    npiece = 0
    for l in range(L):
        for pc in range(24):
            st = wm_stage[npiece % 2]
            src = wmod_d[l, :, pc * 512:(pc + 1) * 512].rearrange("(k p) c -> p k c", p=128)
            for k0 in range(0, KC, 4):
                dma("sp" if npiece % 2 == 0 else "act", st[:, k0:k0 + 4, :], src[:, k0:k0 + 4, :], g_wm[npiece % 2],
                    w=[st[:, k0:k0 + 4, :]])
            g_wm[npiece % 2].close()
            eng = "dve"
            ts(eng, macc, st[:, 0, :], cact[:, 0:1], None, ALU.mult)
            for kc in range(1, KC):
                stt(eng, macc, st[:, kc, :], cact[:, kc:kc + 1], macc, ALU.mult, ALU.add)
            pb = bank(4 + (npiece % 2), 4)
            for j in range(4):
                mm(pb[:, j:j + 1], macc[:, j * 128:(j + 1) * 128], consts[:, C_ONES, 0:1])
            jj = pc * 4
            dst = modT[:, l, :, :].rearrange("p a b -> p (a b)")[:, jj:jj + 4]
            bm = sm("bmod")[:, l * 96 + jj:l * 96 + jj + 4]
            tt("dve", dst, pb, bm, ALU.add)
            npiece += 1
        if l + 1 < L:
            cast_layer(l + 1, defer=True)
    for l in range(L):
        for wch, (scl, gn) in enumerate(((1, "n1g"), (4, "n2g"))):
            g = sm(gn)[:, l * 16:(l + 1) * 16]
            stt("dve", scalA[:, l, wch, :], modT[:, l, scl, :], 1.0, g, ALU.add, ALU.mult)
            ts("dve", scalA[:, l, wch, :], scalA[:, l, wch, :], float(np.sqrt(D)), None, ALU.mult)
    dump("modT", modT, [L, 6, 16])
    dump("scalA", scalA, [L, 2, 16])

    g_w = [S.group("w%d" % i) for i in range(NWB)]
    wctr = [0]

    def wload(src_ap, shape, keyr):
        i = wctr[0] % NWB
        wctr[0] += 1
        a, b = shape
        dst = wbuf[i][:, 0:a * b].rearrange("p (a b) -> p a b", a=a)
        dma("sp", dst, src_ap, g_w[i], w=[dst], kr=[keyr])
        return dst

    def fm_norm(l, wch, dst_bf, tmpbase):
        sq = A.at(tmpbase, [4, TT], BF16)
        rb = A.at(tmpbase + 4096, [TT], F32)
        tmp = A.at(tmpbase + 6144, [2, TT], F32)
        pb = bank(0)
        for k4 in range(4):
            act(sq, xT[:, k4 * 4:(k4 + 1) * 4, :], AF.Square)
            for j in range(4):
                kc = k4 * 4 + j
                mm(pb, onesb, sq[:, j, :], start=(kc == 0), stop=(kc == KC - 1))
        act(rb, pb, AF.Ln, bias=float(D * EPS), scale=1.0)
        act(rb, rb, AF.Exp, scale=-0.5)
        for kc in range(KC):
            t_ = tmp[:, kc % 2, :]
            tt("dve" if kc % 2 == 0 else "pool", t_, xT[:, kc, :], rb, ALU.mult)
            if l is None:
                g = sm("fng")[:, kc:kc + 1]
                ts("dve", dst_bf[:, kc, :], t_, g, float(np.sqrt(D)), ALU.mult, ALU.mult)
            else:
                act(dst_bf[:, kc, :], t_, AF.Identity, bias=modT[:, l, 3 * wch, kc:kc + 1],
                    scale=scalA[:, l, wch, kc:kc + 1])

    g_xin = [S.group("xin%d" % i) for i in range(2)]
    g_out = [S.group("xout%d" % i) for i in range(2)]
    out_ops = []
    R = RBASE

    def tile_head(it):
        t0 = it * TT
        for b in range(NBLK):
            stg = A.at(R + (b % 2) * 8192, [D], F32)
            dma("sp", stg, x_d[t0 + b * 128:t0 + (b + 1) * 128, :], g_xin[b % 2], w=[stg])
            for k4 in range(4):
                pb = bank(k4 % 4)
                for j in range(4):
                    kc = k4 * 4 + j
                    tr(pb[:, j * 128:(j + 1) * 128], stg[:, kc * 128:(kc + 1) * 128], C(C_IDENT))
                cp("dve" if k4 % 2 == 0 else "act", xT[:, k4 * 4:(k4 + 1) * 4, b * 128:(b + 1) * 128],
                   pb.rearrange("p (a b) -> p a b", a=4))
        posi = A.at(R + 16384, [TT], I32)
        posf = A.at(R + 16384 + 2048, [TT], F32)
        ang = A.at(R + 16384 + 4096, [TT], F32)
        kk_i = A.at(R + 16384 + 6144, [TT], I32)
        kk_f = A.at(R + 16384 + 8192, [TT], F32)
        mtmp = A.at(R + 16384 + 10240, [TT], F32)
        dma("sp", posi, pos_d[:, t0:t0 + TT], g_pos, w=[posi])
        cp("dve", posf, posi)
        invf = sm("cvec")[:, 0:1]
        for (dst, shift) in ((sinT, 0.0), (cosT, float(np.pi / 2))):
            ts("dve", ang, posf, invf, shift, ALU.mult, ALU.add)
            ts("dve", mtmp, ang, float(1.0 / (2 * np.pi)), None, ALU.mult)
            cp("dve", kk_i, mtmp)
            cp("dve", kk_f, kk_i)
            stt("dve", ang, kk_f, float(-2 * np.pi), ang, ALU.mult, ALU.add)
            ts("dve", mtmp, ang, float(np.pi), float(2 * np.pi), ALU.is_gt, ALU.mult)
            tt("dve", ang, ang, mtmp, ALU.subtract)
            ts("dve", mtmp, ang, float(-np.pi), float(2 * np.pi), ALU.is_lt, ALU.mult)
            tt("dve", ang, ang, mtmp, ALU.add)
            act(dst, ang, AF.Sin)
        dump("cos%d" % it, cosT, [TT])
        dump("sin%d" % it, sinT, [TT])
        dump("xT%d" % it, xT, [KC, TT])

    bigctr = [0]

    bigmod = [4]

    def bigbank():
        b = bigctr[0] % bigmod[0]
        bigctr[0] += 1
        return bank(b)

    def proj_fm(wb, c0, m, consume, pbase=0, prows=128):
        pb = bigbank()[pbase:pbase + m, :]
        for kc in range(KC):
            mm(pb, wb[:, kc, c0:c0 + m], hT[:, kc, :], start=(kc == 0), stop=(kc == KC - 1))
        consume(pb)

    def proj_tm(wb, b, c0, n, consume):
        pb = bigbank()[:, 0:n]
        for kc in range(KC):
            mm(pb, hT[:, kc, b * 128:(b + 1) * 128], wb[:, kc, c0:c0 + n], start=(kc == 0), stop=(kc == KC - 1))
        consume(pb)

    MIXT_OFF = 0
    MIX_OFF = 16384
    MB_OFF = 24576

    def mix_to_T(mix, b, ncols, chunk0):
        mixT = A.at(R + MIXT_OFF, [KC, TT], BF16)
        for c4 in range(0, ncols // 128, 4):
            pbb = bigbank().bitcast(BF16)[:, 0:512]
            for j in range(4):
                tr(pbb[:, j * 128:(j + 1) * 128], mix[:, b, (c4 + j) * 128:(c4 + j + 1) * 128], identb)
            cp("dve", mixT[:, chunk0 + c4:chunk0 + c4 + 4, b * 128:(b + 1) * 128],
               pbb.rearrange("p (a b) -> p a b", a=4))

    def swa(l, it):
        MB = R + MB_OFF
        mix = A.at(R + MIX_OFF, [NBLK, 1024], BF16)
        NQB = 4
        q32 = A.at(MB, [NQB, TT], F32)
        qT = A.at(MB + 8192, [16, TT], BF16)
        kT = A.at(MB + 24576, [2, 640], BF16)
        vaug = A.at(MB + 27136, [5, 2, 66], BF16)
        rt = A.at(MB + 28480, [NQB, 2, TT], F32)
        pT = [A.at(MB + 28480 + i * 4096, [2, 8, 128], BF16) for i in range(2)]
        den = A.at(MB + 44864, [2, 8], F32)
        rcp = A.at(MB + 44864 + 64, [2, 8], F32)
        assert MB + 44864 + 128 <= A.nbytes
        cp("pool", kT[:, :, 0:128], kcar[l])
        cp("pool", vaug[:, 0, :, :], vcar[l])
        memset("pool", vaug[:, 1:5, :, 64:65], 1.0)
        qTe = qT.rearrange("p (c e) t -> p c e t", e=2)
        memset("pool", qTe[0:64, :, 1, :], 0.0)
        memset("pool", qTe[64:128, :, 0, :], 0.0)
        ci = [0]
        for bi, name in enumerate(["bq0", "bq1", "bq2", "bq3", "bk"]):
            v, w = in_view(l, name)
            wb = wload(v, (KC, w), ("sc_in", l))
            first_pbs = None
            if bi == 0:
                first_pbs = [bigbank() for _ in range(2)]
                for kc in range(KC):
                    for c in range(2):
                        mm(first_pbs[c], wb[:, kc, c * 128:(c + 1) * 128], hT[:, kc, :], start=(kc == 0), stop=(kc == KC - 1))
            for c in range(2):
                def consume(pb, bi=bi, c=c):
                    k_ = ci[0] % NQB
                    q_ = q32[:, k_, :]
                    ci[0] += 1
                    cp("act", q_, pb)
                    rp = bank(4 + k_)
                    mm(rp, C(C_PROT), q_)
                    tt("pool", rt[:, k_, 0, :], q_, cosT, ALU.mult)
                    tt("dve", rt[:, k_, 1, :], rp, sinT, ALU.mult)
                    if bi < 4:
                        ch = bi * 2 + c
                        tt("pool", qT[0:64, 2 * ch, :], rt[0:64, k_, 0, :], rt[0:64, k_, 1, :], ALU.add)
                        tt("pool", qT[64:128, 2 * ch + 1, :], rt[64:128, k_, 0, :], rt[64:128, k_, 1, :], ALU.add)
                    else:
                        tt("pool", kT[:, c, 128:640], rt[:, k_, 0, :], rt[:, k_, 1, :], ALU.add)
                if first_pbs is not None:
                    consume(first_pbs[c])
                else:
                    proj_fm(wb, c * 128, 128, consume)
        v, w = in_view(l, "bv")
        wb = wload(v, (KC, 128), ("sc_in", l))
        for b in range(NBLK):
            proj_tm(wb, b, 0, 128, lambda pb, b=b: cp("act", vaug[:, 1 + b, :, 0:64],
                                                      pb.rearrange("p (g d) -> p g d", g=2)))
        dump("qT_l%d_t%d" % (l, it), qT, [16, TT])
        dump("kT_l%d_t%d" % (l, it), kT, [2, 640])
        dump("vaug_l%d_t%d" % (l, it), vaug, [5, 2, 66])
        if stop_after == "swa_proj":
            return
        n = 0
        for b in range(NBLK):
            for g in range(2):
                pt = pT[n % 2]
                for jb in range(2):
                    sp_ = ps[:, 2048 + jb * 1024:2048 + (jb + 1) * 1024]
                    kcols = kT[:, g, (b + jb) * 128:(b + jb + 1) * 128]
                    for hh in range(8):
                        mm(sp_[:, hh * 128:(hh + 1) * 128], kcols,
                           qT[:, g * 8 + hh, b * 128:(b + 1) * 128])
                    if stop_after == "swa_c1":
                        return
                    for hb in range(2):
                        act(pt[:, jb, hb * 4:(hb + 1) * 4, :],
                            sp_[:, hb * 512:(hb + 1) * 512].rearrange("p (h q) -> p h q", h=4), AF.Exp, scale=0.125)
                    if stop_after == "swa_c2":
                        return
                    mk = maskPb if jb == 0 else maskCb
                    tt("pool", pt[:, jb, :, :], pt[:, jb, :, :], mk.unsqueeze(1).to_broadcast([128, 8, 128]), ALU.mult)
                if stop_after == "swa_c3":
                    return
                ob = ps[:, (n % 2) * 1024:(n % 2) * 1024 + 1024].rearrange("p (h c) -> p h c", h=8)
                for hh in range(8):
                    for jb in range(2):
                        mm(ob[:, hh, 0:65], pt[:, jb, hh, :], vaug[:, b + jb, g, 0:65], start=(jb == 0), stop=(jb == 1))
                if stop_after == "swa_c4":
                    return
                dn = den[:, n % 2, :]
                rc = rcp[:, n % 2, :]
                tt("dve", dn, ob[:, :, 64], esink[:, l, g * 8:(g + 1) * 8], ALU.add)
                S.add("dve", lambda e, rc=rc, dn=dn: e.reciprocal(rc, dn), r=[dn], w=[rc])
                tt("dve", mix[:, b, g * 512:(g + 1) * 512].rearrange("p (h d) -> p h d", h=8), ob[:, :, 0:64],
                   rc.unsqueeze(2).to_broadcast([128, 8, 64]), ALU.mult)
                n += 1
                if stop_after == "swa_c5":
                    return
            mix_to_T(mix, b, 1024, 4)
            if stop_after == "swa_c6":
                return
        cp("pool", kcar[l], kT[:, :, 512:640])
        cp("pool", vcar[l], vaug[:, 4, :, :])
        dump("mixB_l%d_t%d" % (l, it), mix, [NBLK, 1024])

    def gla(l, it):
        MB = R + MB_OFF
        mix = A.at(R + MIX_OFF, [NBLK, 512], BF16)
        aqT = A.at(MB, [4, TT], F32)
        akT = A.at(MB + 8192, [4, TT], F32)
        alrT = A.at(MB + 16384, [TT], F32)
        av = A.at(MB + 18432, [NBLK, 512], BF16)
        sga = A.at(MB + 22528, [NBLK, 512], F32)
        e_ = A.at(MB + 30720, [256], F32)
        l_ = A.at(MB + 31744, [256], F32)
        ebT = A.at(MB + 32768, [4, 128], F32)
        enbT = A.at(MB + 34816, [4, 128], F32)
        qtT = A.at(MB + 37888, [4, 128], BF16)
        ktT = A.at(MB + 38912, [4, 128], BF16)
        kt = A.at(MB + 39936, [4, 64], BF16)
        AmT = A.at(MB + 40448, [4, 128], BF16)
        osq = A.at(MB + 41472, [512], F32)
        ss = A.at(MB + 43520, [4], F32)
        rstd = A.at(MB + 43584, [4], F32)
        og = A.at(MB + 43648, [512], F32)
        gain = sm("gagain")[:, l * 128:(l + 1) * 128]
        wgk = sm("wgk")[0:17, l * 256:(l + 1) * 256]
        memset("pool", alrT[0:32, :], 1.0)
        for name, dstT in (("aq", aqT), ("ak", akT)):
            v, w = in_view(l, name)
            wb = wload(v, (KC, 256), ("sc_in", l))
            for h in range(4):
                proj_fm(wb, h * 64, 64, lambda pb, h=h, dstT=dstT: cp("act", dstT[0:64, h, :], pb))
        v, w = in_view(l, "alr")
        wb = wload(v, (KC, 16), ("sc_in", l))
        proj_fm(wb, 0, 16, lambda pb: cp("act", alrT[0:16, :], pb))
        for half in range(2):
            v, w = in_view(l, "av%d" % half)
            wb = wload(v, (KC, 256), ("sc_in", l))
            for b in range(NBLK):
                proj_tm(wb, b, 0, 256, lambda pb, b=b, half=half: cp("act", av[:, b, half * 256:(half + 1) * 256], pb))
        for half in range(2):
            v, w = in_view(l, "ag%d" % half)
            wb = wload(v, (KC, 256), ("sc_in", l))
            for b in range(NBLK):
                def consume(pb, b=b, half=half):
                    d_ = sga[:, b, half * 256:(half + 1) * 256]
                    act(d_, pb, AF.Silu)
                    d3 = d_.rearrange("p (h d) -> p h d", h=2)
                    stt("dve", d3, d3, float(np.sqrt(128.0)), gain.unsqueeze(1).to_broadcast([128, 2, 128]),
                        ALU.mult, ALU.mult)
                proj_tm(wb, b, 0, 256, consume)
        Sl = glaS[l]
        Sb = glaSb[l]
        for b in range(NBLK):
            bc = slice(b * 128, (b + 1) * 128)
            pp = bank(4)[:, 0:256]
            mm(pp, alrT[0:17, bc], wgk)
            act(e_, pp, AF.Exp, scale=-1.0)
            act(l_, e_, AF.Ln, bias=1.0)
            pbT = bank(5)[0:64, :]
            for h in range(4):
                mm(pbT[:, h * 128:(h + 1) * 128], l_[:, h * 64:(h + 1) * 64], C(C_TRIN))
            pb3 = pbT.rearrange("p (h t) -> p h t", h=4)
            act(ebT[0:64], pb3, AF.Exp)
            act(enbT[0:64], pb3, AF.Exp, scale=-1.0)
            stt("dve", qtT[0:64], aqT[0:64, :, bc], 0.125, ebT[0:64], ALU.mult, ALU.mult)
            tt("pool", ktT[0:64], akT[0:64, :, bc], enbT[0:64], ALU.mult)
            pk = bank(6).bitcast(BF16)
            for h in range(4):
                tr(pk[:, h * 64:(h + 1) * 64], ktT[0:64, h, :], identb[0:64, 0:64])
            cp("dve", kt, pk[:, 0:256].rearrange("p (h d) -> p h d", h=4))
            pa = bank(7)
            for h in range(4):
                mm(pa[:, h * 128:(h + 1) * 128], ktT[0:64, h, :], qtT[0:64, h, :])
            tt("dve", AmT, pa.rearrange("p (h t) -> p h t", h=4), C(C_TRII).unsqueeze(1).to_broadcast([128, 4, 128]),
               ALU.mult)
            po = bank(4)
            for h in range(4):
                mm(po[:, h * 128:(h + 1) * 128], AmT[:, h, :], av[:, b, h * 128:(h + 1) * 128], start=True, stop=False)
                mm(po[:, h * 128:(h + 1) * 128], qtT[0:64, h, :], Sb[0:64, h, :], start=False, stop=True)
            act(osq, po, AF.Square)
            S.add("dve", lambda e: e.tensor_reduce(ss, osq.rearrange("p (h d) -> p h d", h=4), AX.X, ALU.add),
                  r=[osq], w=[ss])
            act(rstd, ss, AF.Ln, bias=float(128 * EPS))
            act(rstd, rstd, AF.Exp, scale=-0.5)
            tt("dve", og.rearrange("p (h d) -> p h d", h=4), po.rearrange("p (h d) -> p h d", h=4),
               rstd.unsqueeze(2).to_broadcast([128, 4, 128]), ALU.mult)
            tt("pool", mix[:, b, :], og, sga[:, b, :], ALU.mult)
            pP = bank(5)[0:64, :]
            for h in range(4):
                mm(pP[:, h * 128:(h + 1) * 128], kt[:, h, :], av[:, b, h * 128:(h + 1) * 128])
            tt("dve", Sl[0:64], Sl[0:64], pP.rearrange("p (h d) -> p h d", h=4), ALU.add)
            tt("dve", Sl[0:64], Sl[0:64], ebT[0:64, :, 127:128].to_broadcast([64, 4, 128]), ALU.mult)
            cp("act", Sb[0:64], Sl[0:64])
            mix_to_T(mix, b, 512, 0)
        dump("mixA_l%d_t%d" % (l, it), mix, [NBLK, 512])

    def gdn(l, it):
        MB = R + MB_OFF
        mix = A.at(R + MIX_OFF, [NBLK, 512], BF16)
        cin = A.at(MB, [2, 516], F32)
        cacc = A.at(MB + 4128, [2, TT], F32)
        sqb = A.at(MB + 8224, [2, TT], BF16)
        rn = A.at(MB + 10272, [2, TT], F32)
        hs0_ = MB + 30880 + 640 + 4096
        cin2 = A.at(hs0_, [2, 516], F32)
        cacc2 = A.at(hs0_ + 4128, [2, TT], F32)
        sqb2 = A.at(hs0_ + 8224, [2, TT], BF16)
        rn2 = A.at(hs0_ + 10272, [2, TT], F32)
        NCB = 4
        cinb = [cin[:, 0, :], cin[:, 1, :], cin2[:, 0, :], cin2[:, 1, :]]
        caccb = [cacc[:, 0, :], cacc[:, 1, :], cacc2[:, 0, :], cacc2[:, 1, :]]
        sqbb = [sqb[:, 0, :], sqb[:, 1, :], sqb2[:, 0, :], sqb2[:, 1, :]]
        rnb = [rn[:, 0, :], rn[:, 1, :], rn2[:, 0, :], rn2[:, 1, :]]
        qnT = A.at(MB + 14368, [4, TT], BF16)
        knT = A.at(MB + 18464, [4, TT], BF16)
        cvT = A.at(MB + 22560, [4, TT], BF16)
        sgz = A.at(MB + 26656, [NBLK, 512], BF16)
        bca = A.at(MB + 30752, [NBLK, 8], F32)
        sc0 = MB + 30880
        e1 = A.at(sc0, [4], F32)
        l1 = A.at(sc0 + 64, [4], F32)
        beta = A.at(sc0 + 128, [4], F32)
        t2 = A.at(sc0 + 192, [4], F32)
        l2 = A.at(sc0 + 256, [4], F32)
        g_ = A.at(sc0 + 320, [4], F32)
        eg3 = A.at(sc0 + 384, [3, 4], F32)
        ngc = A.at(sc0 + 448, [4], F32)
        bgk = A.at(sc0 + 512, [4], F32)
        nl1 = A.at(sc0 + 576, [4], F32)
        Gbc = A.at(sc0 + 640, [4, 128], F32)
        LBbc = A.at(sc0 + 640 + 2048, [4, 128], F32)
        hs0 = sc0 + 640 + 4096
        LT = A.at(MB, [4, 128], F32)
        Mj = A.at(MB + 2048, [2, 4, 128], F32)
        Tm = A.at(MB + 6144, [4, 128], F32)
        Ym = A.at(MB + 8192, [4, 128], F32)
        Wm = A.at(MB + 10240, [4, 128], F32)
        Yb = A.at(MB + 12288, [4, 128], BF16)
        attnT = A.at(MB + 13312, [4, 128], BF16)
        EE = A.at(hs0, [4, 2, 128], F32)
        EG = A.at(hs0 + 4096, [4, 128], F32)
        qgT = A.at(hs0 + 6144, [4, 128], BF16)
        vb = A.at(hs0 + 7168, [4, 128], BF16)
        kbg = A.at(hs0 + 8192, [4, 128], BF16)
        kdec = A.at(hs0 + 9216, [4, 128], BF16)
        wT = A.at(hs0 + 10240, [4, 128], BF16)
        vnew = A.at(hs0 + 11264, [4, 128], BF16)
        ub = A.at(hs0 + 12288, [4, 128], F32)
        osq = A.at(hs0 + 14336, [512], F32)
        og = A.at(hs0 + 16384, [512], F32)
        ss = A.at(hs0 + 18432, [4], F32)
        rstd = A.at(hs0 + 18496, [4], F32)
        assert hs0 + 18560 <= A.nbytes, (hs0 + 18560, A.nbytes)
        gain = sm("gcgain")[:, l * 128:(l + 1) * 128]
        convw = sm("convw")[:, l * 48:(l + 1) * 48].rearrange("p (j c) -> p j c", j=4)
        n = 0
        for bi, name in enumerate(["cq0", "cq1", "ck0", "ck1", "cv0", "cv1"]):
            v, w = in_view(l, name)
            wb = wload(v, (KC, 256), ("sc_in", l))
            for c in range(2):
                ch = bi * 2 + c

                def consume(pb, ch=ch):
                    nonlocal n
                    ci_ = cinb[n % NCB]
                    ca_ = caccb[n % NCB]
                    cp("pool", ci_[:, 0:3], ccar[l][:, ch, 0:3])
                    cp("act", ci_[:, 3:515], pb)
                    cp("pool", ccar[l][:, ch, 0:3], ci_[:, 512:515])
                    ts("dve", ca_, ci_[:, 0:512], convw[:, 0, ch:ch + 1], None, ALU.mult)
                    for j in range(1, 4):
                        stt("dve", ca_, ci_[:, j:j + 512], convw[:, j, ch:ch + 1], ca_, ALU.mult, ALU.add)
                    act(ca_, ca_, AF.Silu)
                    if ch < 8:
                        sq_ = sqbb[n % NCB]
                        rn_ = rnb[n % NCB]
                        act(sq_, ca_, AF.Square)
                        pbs = bank(4 + n % NCB)
                        mm(pbs, onesb, sq_)
                        act(rn_, pbs, AF.Ln, bias=float(EPS))
                        act(rn_, rn_, AF.Exp, scale=-0.5)
                        if ch < 4:
                            stt("dve", qnT[:, ch, :], ca_, float(128.0 ** -0.5), rn_, ALU.mult, ALU.mult)
                        else:
                            tt("dve", knT[:, ch - 4, :], ca_, rn_, ALU.mult)
                    else:
                        cp("dve", cvT[:, ch - 8, :], ca_)
                    n += 1
                proj_fm(wb, c * 128, 128, consume)
        for half in range(2):
            v, w = in_view(l, "cz%d" % half)
            wb = wload(v, (KC, 256), ("sc_in", l))
            for b in range(NBLK):
                def consume(pb, b=b, half=half):
                    tmpz = caccb[(b * 2 + half) % NCB][:, 0:256]
                    act(tmpz, pb, AF.Silu)
                    stt("dve", sgz[:, b, half * 256:(half + 1) * 256].rearrange("p (h d) -> p h d", h=2),
                        tmpz.rearrange("p (h d) -> p h d", h=2), float(np.sqrt(128.0)),
                        gain.unsqueeze(1).to_broadcast([128, 2, 128]), ALU.mult, ALU.mult)
                proj_tm(wb, b, 0, 256, consume)
        v, w = in_view(l, "cbca")
        wb = wload(v, (KC, 8), ("sc_in", l))
        for b in range(NBLK):
            proj_tm(wb, b, 0, 8, lambda pb, b=b: cp("act", bca[:, b, :], pb))
        dump("qnT_l%d_t%d" % (l, it), qnT, [4, TT])
        dump("knT_l%d_t%d" % (l, it), knT, [4, TT])
        dump("cvT_l%d_t%d" % (l, it), cvT, [4, TT])
        Sl = gdnS[l]
        Sb = gdnSb[l]
        I_ = C(C_IDENT)
        bigmod[0] = 3
        for b in range(NBLK):
            bc = slice(b * 128, (b + 1) * 128)
            act(e1, bca[:, b, 0:4], AF.Exp, scale=-1.0)
            act(l1, e1, AF.Ln, bias=1.0)
            act(beta, l1, AF.Exp, scale=-1.0)
            ts("dve", nl1, l1, -1.0, None, ALU.mult)
            tt("dve", t2, bca[:, b, 4:8], sm("dtb")[:, l * 4:(l + 1) * 4], ALU.add)
            act(t2, t2, AF.Exp)
            act(l2, t2, AF.Ln, bias=1.0)
            tt("dve", g_, l2, aexp[:, l, :], ALU.mult)
            pg = bank(7)[:, 0:12]
            mm(pg[:, 0:4], C(C_TRII), g_)
            mm(pg[:, 4:8], C(C_TRIR), g_)
            mm(pg[:, 8:12], C(C_ONES), g_)
            act(eg3, pg.rearrange("p (a h) -> p a h", a=3), AF.Exp)
            ts("dve", ngc, pg[:, 0:4], -1.0, None, ALU.mult)
            tt("dve", bgk, beta, eg3[:, 0, :], ALU.mult)
            cp("dve", Gbc, g_.unsqueeze(2).to_broadcast([128, 4, 128]))
            cp("pool", LBbc, nl1.unsqueeze(2).to_broadcast([128, 4, 128]))
            po = bank(3)
            v4 = lambda ap: ap.rearrange("p (h t) -> p h t", h=4)
            hsl = lambda ap, h: ap[:, h * 128:(h + 1) * 128]
            I4 = I_.unsqueeze(1).to_broadcast([128, 4, 128])
            pgc = bank(4)
            for h in range(4):
                mm(hsl(pgc, h), Gbc[:, h, :], C(C_TRII))
            for h in range(4):
                pe1 = ps[:, 2560 + h * 256:2560 + h * 256 + 128]
                pe0 = ps[:, 2560 + h * 256 + 128:2560 + (h + 1) * 256]
                mm(pe1, Gbc[:, h, :], C(C_TRII), start=True, stop=False)
                mm(pe1, LBbc[:, h, :], I_, start=False, stop=False)
                mm(pe1, I_, C(C_MN1), start=False, stop=True)
                mm(pe0, Gbc[:, h, :], C(C_TRII), start=True, stop=False)
                mm(pe0, I_, C(C_MN0), start=False, stop=True)
            act(EG, v4(pgc), AF.Exp)
            for h in range(4):
                act(EE[:, h, :, :], ps[:, 2560 + h * 256:2560 + (h + 1) * 256].rearrange("p (a t) -> p a t", a=2),
                    AF.Exp, bias=ngc[:, h:h + 1])
            tt("pool", qgT, qnT[:, :, bc], EG, ALU.mult)
            pgk = bank(7)
            pgq = bank(4)
            for h in range(4):
                mm(hsl(pgk, h), knT[:, h, bc], knT[:, h, bc])
            for h in range(4):
                mm(hsl(pgq, h), knT[:, h, bc], qnT[:, h, bc])
            tt("dve", LT, v4(pgk), EE[:, :, 0, :], ALU.mult)
            tt("dve", attnT, v4(pgq), EE[:, :, 1, :], ALU.mult)
            lm = lambda j: C(C_LVL + j - 1).unsqueeze(1).to_broadcast([128, 4, 128])
            tt("pool", Mj[:, 0], LT, lm(1), ALU.mult)
            tt("dve", Ym, I4, Mj[:, 0], ALU.subtract)
            pz = bank(5)
            pyn = bank(6)
            ptn = bank(7)
            for h in range(4):
                tr(hsl(pz, h), Ym[:, h, :], I_)
            cp("act", Tm, v4(pz))
            for j in range(2, 8):
                mj = Mj[:, j % 2]
                tt("pool", mj, LT, lm(j), ALU.mult)
                for h in range(4):
                    mm(hsl(pz, h), mj[:, h, :], Tm[:, h, :])
                tt("dve", Wm, I4, v4(pz), ALU.subtract)
                for h in range(4):
                    mm(hsl(pyn, h), Wm[:, h, :], Ym[:, h, :])
                if j < 7:
                    for h in range(4):
                        mm(hsl(ptn, h), Ym[:, h, :], Wm[:, h, :])
                    cp("dve", Ym, v4(pyn))
                    cp("act", Tm, v4(ptn))
                else:
                    cp("act", Yb, v4(pyn))
            ptr = bank(4).bitcast(BF16)
            for h in range(4):
                tr(ptr[:, h * 128:(h + 1) * 128], cvT[:, h, bc], identb)
                tr(ptr[:, 512 + h * 128:512 + (h + 1) * 128], knT[:, h, bc], identb)
            tt("dve", vb, v4(ptr[:, 0:512]), beta.unsqueeze(2).to_broadcast([128, 4, 128]), ALU.mult)
            tt("dve", kbg, v4(ptr[:, 512:1024]), bgk.unsqueeze(2).to_broadcast([128, 4, 128]), ALU.mult)
            tt("dve", kdec, v4(ptr[:, 512:1024]), eg3[:, 1, :].unsqueeze(2).to_broadcast([128, 4, 128]), ALU.mult)
            pu = bank(5)
            pwt = bank(6)
            for h in range(4):
                mm(hsl(pu, h), Yb[:, h, :], vb[:, h, :])
            for h in range(4):
                mm(hsl(pwt, h), kbg[:, h, :], Yb[:, h, :])
            cp("act", ub, v4(pu))
            cp("dve", wT, v4(pwt))
            pvn = bank(7)
            psn = bank(4)
            for h in range(4):
                mm(hsl(pvn, h), wT[:, h, :], Sb[:, h, :])
            tt("dve", vnew, ub, v4(pvn), ALU.subtract)
            for h in range(4):
                mm(hsl(po, h), qgT[:, h, :], Sb[:, h, :], start=True, stop=False)
                mm(hsl(po, h), attnT[:, h, :], vnew[:, h, :], start=False, stop=True)
            for h in range(4):
                mm(hsl(psn, h), kdec[:, h, :], vnew[:, h, :])
            tt("pool", Sl, Sl, eg3[:, 2, :].unsqueeze(2).to_broadcast([128, 4, 128]), ALU.mult)
            tt("dve", Sl, Sl, v4(psn), ALU.add)
            cp("act", Sb, Sl)
            act(osq, po, AF.Square)
            S.add("dve", lambda e: e.tensor_reduce(ss, osq.rearrange("p (h d) -> p h d", h=4), AX.X, ALU.add),
                  r=[osq], w=[ss])
            act(rstd, ss, AF.Ln, bias=float(128 * EPS))
            act(rstd, rstd, AF.Exp, scale=-0.5)
            tt("dve", og.rearrange("p (h d) -> p h d", h=4), po.rearrange("p (h d) -> p h d", h=4),
               rstd.unsqueeze(2).to_broadcast([128, 4, 128]), ALU.mult)
            tt("pool", mix[:, b, :], og, sgz[:, b, :], ALU.mult)
            mix_to_T(mix, b, 512, 12)
        bigmod[0] = 4
        dump("mixC_l%d_t%d" % (l, it), mix, [NBLK, 512])

    def out_proj(l, it):
        mixT = A.at(R + MIXT_OFF, [KC, TT], BF16)
        for blk in range(8):
            wb = wload(sc_out[l][blk], (KC, 256), ("sc_out", l))
            for c in range(2):
                m = blk * 2 + c
                pb = bigbank()
                for kc in range(KC):
                    mm(pb, wb[:, kc, c * 128:(c + 1) * 128], mixT[:, kc, :], start=(kc == 0), stop=(kc == KC - 1))
                stt("dve", xT[:, m, :], pb, modT[:, l, 2, m:m + 1], xT[:, m, :], ALU.mult, ALU.add)

    def ffn(l, it):
        aT = A.at(R, [FC, TT], BF16)
        ftmp = A.at(R + 45056, [2, TT], F32)
        fm_norm(l, 1, hT, R + 49152)
        for f2 in range(22):
            wgb = wload(sc_g[l][f2], (KC, 256), ("sc_g", l))
            wub = wload(sc_u[l][f2], (KC, 256), ("sc_u", l))
            if f2 == 0:
                bks = [bigbank() for _ in range(4)]
                for kc in range(KC):
                    for c in range(2):
                        mm(bks[2 * c], wgb[:, kc, c * 128:(c + 1) * 128], hT[:, kc, :], start=(kc == 0), stop=(kc == KC - 1))
                        mm(bks[2 * c + 1], wub[:, kc, c * 128:(c + 1) * 128], hT[:, kc, :], start=(kc == 0), stop=(kc == KC - 1))
                for c in range(2):
                    ft = ftmp[:, c % 2, :]
                    act(ft, bks[2 * c], AF.Silu)
                    tt("dve", aT[:, c, :], ft, bks[2 * c + 1], ALU.mult)
                continue
            for c in range(2):
                f = f2 * 2 + c
                pg_ = bigbank()
                for kc in range(KC):
                    mm(pg_, wgb[:, kc, c * 128:(c + 1) * 128], hT[:, kc, :], start=(kc == 0), stop=(kc == KC - 1))
                pu_ = bigbank()
                for kc in range(KC):
                    mm(pu_, wub[:, kc, c * 128:(c + 1) * 128], hT[:, kc, :], start=(kc == 0), stop=(kc == KC - 1))
                ft = ftmp[:, f % 2, :]
                act(ft, pg_, AF.Silu)
                tt("dve", aT[:, f, :], ft, pu_, ALU.mult)
        for m in range(16):
            pb = bigbank()
            for kh in range(2):
                wb = wload(sc_d[l][m * 2 + kh], (22, 128), ("sc_d", l))
                for k in range(22):
                    mm(pb, wb[:, k, :], aT[:, kh * 22 + k, :], start=(kh == 0 and k == 0), stop=(kh == 1 and k == 21))
            stt("dve", xT[:, m, :], pb, modT[:, l, 5, m:m + 1], xT[:, m, :], ALU.mult, ALU.add)

    def tile_tail(it):
        t0 = it * TT
        fm_norm(None, 0, xT, R + 49152)
        for b in range(NBLK):
            ostage = A.at(R + (b % 2) * 8192, [D], F32)
            for k4 in range(4):
                pb = bigbank()
                for j in range(4):
                    tr(pb[:, j * 128:(j + 1) * 128], xT[:, k4 * 4 + j, b * 128:(b + 1) * 128], C(C_IDENT))
                cp("dve" if k4 % 2 == 0 else "act", ostage[:, k4 * 512:(k4 + 1) * 512], pb)
            out_ops.append(dma("sp", out_d[t0 + b * 128:t0 + (b + 1) * 128, :], ostage, g_out[b % 2], r=[ostage]))

    def finish():
        if SCHEDULE:
            S.schedule()
        S.emit(out_ops + dump_ops)
        return nc

    stop_after = stop_after or "none"
    for it in range(NT):
        tile_head(it)
        for l in range(L):
            ec = (lambda i: emit_casts(l + 1, i, 6)) if it == 0 else (lambda i: None)
            ec(0)
            fm_norm(l, 0, hT, R + MB_OFF)
            dump("hT_l%d_t%d" % (l, it), hT, [KC, TT])
            if stop_after == "norm":
                return finish()
            ec(1)
            swa(l, it)
            if stop_after.startswith("swa"):
                return finish()
            ec(2)
            gla(l, it)
            if stop_after == "gla":
                return finish()
            ec(3)
            gdn(l, it)
            if stop_after == "gdn":
                return finish()
            ec(4)
            out_proj(l, it)
            dump("xTmid_l%d_t%d" % (l, it), xT, [KC, TT])
            ec(5)
            ffn(l, it)
            dump("xTend_l%d_t%d" % (l, it), xT, [KC, TT])
        tile_tail(it)
    return finish()


_NC_CACHE = {}


def kernel(x, c, positions, w_mod, b_mod, norm1_gain, norm2_gain, w_in, gla_w_gk, gla_b_gk,
           gla_norm_gain, swa_sinks, gdn_conv_w, gdn_a_log, gdn_dt_bias, gdn_norm_gain, w_out,
           ffn_w_gate, ffn_w_up, ffn_w_down, final_norm_gain):
    inp = dict(x=x, c=c, positions=positions, w_mod=w_mod, b_mod=b_mod, norm1_gain=norm1_gain,
               norm2_gain=norm2_gain, w_in=w_in, gla_w_gk=gla_w_gk, gla_b_gk=gla_b_gk,
               gla_norm_gain=gla_norm_gain, swa_sinks=swa_sinks, gdn_conv_w=gdn_conv_w,
               gdn_a_log=gdn_a_log, gdn_dt_bias=gdn_dt_bias, gdn_norm_gain=gdn_norm_gain, w_out=w_out,
               ffn_w_gate=ffn_w_gate, ffn_w_up=ffn_w_up, ffn_w_down=ffn_w_down,
               final_norm_gain=final_norm_gain)
    inp = {k: np.asarray(v) for k, v in inp.items()}
    B, T, _ = inp["x"].shape
    L = inp["w_in"].shape[0]
    nc = build(T, L)
    consts, _ = make_consts()
    f32c = lambda a: np.ascontiguousarray(a, dtype=np.float32)
    shared = {"consts": consts, "w_mod": f32c(inp["w_mod"]), "w_in": f32c(inp["w_in"]), "w_out": f32c(inp["w_out"]),
              "w_gate": f32c(inp["ffn_w_gate"]), "w_up": f32c(inp["ffn_w_up"]), "w_down": f32c(inp["ffn_w_down"])}
    in_maps = []
    for b in range(B):
        m = dict(shared)
        m["x"] = f32c(inp["x"][b])
        m["pos"] = np.ascontiguousarray(np.broadcast_to(inp["positions"][b][None, :], (128, T))).astype(np.int32)
        m["small"] = pack_small(inp, b, L)
        in_maps.append(m)
    res = run_bass_kernel_spmd(nc, in_maps, core_ids=list(range(B)))
    return np.stack([np.asarray(r["out"], dtype=np.float32) for r in res.results], axis=0)
```

```python
import numpy as np
import concourse.bass as bass
import concourse.mybir as mybir
from concourse.bass_utils import run_bass_kernel_spmd

F32 = mybir.dt.float32
BF16 = mybir.dt.bfloat16
I32 = mybir.dt.int32
U8 = mybir.dt.uint8
AF = mybir.ActivationFunctionType
ALU = mybir.AluOpType
AX = mybir.AxisListType
ISZ = {F32: 4, BF16: 2, I32: 4, U8: 1}

D = 2048
KC = 16
DFF = 5632
FC = 44
INW = 4888
EPS = 1e-6
TT = 512
NBLK = 4
SCHEDULE = True
EMBED_WAIT = True
CRITPATH = True


class Op:
    __slots__ = ("eng", "fn", "deps", "odeps", "sig", "val", "grp", "isdma", "q", "cost", "idx", "fin", "nsucc", "succ", "npend", "est", "bl")

    def __init__(self, eng, fn, isdma=False, grp=None, q=None):
        self.eng = eng
        self.fn = fn
        self.deps = []
        self.odeps = []
        self.cost = 100.0
        self.sig = False
        self.val = 0
        self.grp = grp
        self.isdma = isdma
        self.q = q


class DmaGroup:
    def __init__(self, name, wait_total=False):
        self.name = name
        self.count = 0
        self.sem = None
        self.wait_total = wait_total
        self.cur = []

    def close(self):
        for op in self.cur:
            op.val = self.count
        self.cur = []


GRAN = 1024


class Sched:
    ENGS = ("pe", "act", "dve", "pool", "sp")

    def __init__(self, nc):
        self.nc = nc
        self.streams = {e: [] for e in self.ENGS}
        self.acc = {}
        self.keys_w = {}
        self.keys_r = {}
        self.groups = []
        self.nops = 0

    def group(self, name, wait_total=False):
        g = DmaGroup(name, wait_total)
        self.groups.append(g)
        return g

    @staticmethod
    def region(ap):
        sp = str(ap.space)
        isz = ISZ[ap.dtype]
        pstride, pcnt = ap.ap[0]
        off = ap.offset
        if pstride:
            p0 = off // pstride
            f0 = off % pstride
        else:
            p0 = 0
            f0 = off
        ext = 1
        for s, c in ap.ap[1:]:
            ext += abs(s) * (c - 1)
        return (sp + ap.tensor.name, p0, p0 + pcnt, f0 * isz, (f0 + ext) * isz)

    def _dep(self, op, other, a_write, b_write):
        if other is op:
            return
        if other.isdma or op.isdma:
            if other.isdma and op.isdma and other.grp is op.grp and other.q == op.q:
                op.odeps.append(other)
                return
            op.deps.append(other)
            return
        if other.eng == op.eng:
            if op.eng == "pe":
                op.odeps.append(other)
                return
            op.deps.append(other)
            return
        op.deps.append(other)

    def _access(self, op, ap, is_write):
        sp, p0, p1, lo, hi = self.region(ap)
        psum = ap.tensor.name == "ps"
        if psum:
            p0, p1 = 0, 128
            lo = (lo // 2048) * 2048
            hi = ((hi + 2047) // 2048) * 2048
        g0 = lo // GRAN
        g1 = (hi - 1) // GRAN
        for g in range(g0, g1 + 1):
            key = (sp, g)
            lst = self.acc.get(key)
            if lst is None:
                lst = []
                self.acc[key] = lst
            glo = max(lo, g * GRAN)
            ghi = min(hi, (g + 1) * GRAN)
            keep = []
            for e in lst:
                ep0, ep1, elo, ehi, eop, ew = e
                ov = ep0 < p1 and p0 < ep1 and elo < ghi and glo < ehi
                conflict = ov and (is_write or ew or (psum and eop.eng != op.eng))
                if conflict:
                    self._dep(op, eop, is_write, ew)
                if ov and ep0 >= p0 and ep1 <= p1 and elo >= glo and ehi <= ghi:
                    if is_write:
                        continue
                keep.append(e)
            keep.append((p0, p1, glo, ghi, op, is_write))
            self.acc[key] = keep

    def add(self, eng, fn, r=(), w=(), kr=(), kw=(), dma=None, cost=None):
        op = Op(eng, fn, isdma=dma is not None, grp=dma, q=eng)
        op.idx = self.nops
        if cost is None:
            n = 1
            aps = list(w) + list(r)
            if aps:
                for d_ in aps[0].shape[1:]:
                    n *= d_
            cost = 70.0 + n / 0.96 * (1.6 if eng == "pool" else 1.0)
        op.cost = cost
        for ap in r:
            self._access(op, ap, False)
        for ap in w:
            self._access(op, ap, True)
        for k in kr:
            o = self.keys_w.get(k)
            if o is not None:
                self._dep(op, o, False, True)
            self.keys_r.setdefault(k, []).append(op)
        for k in kw:
            o = self.keys_w.get(k)
            if o is not None:
                self._dep(op, o, True, True)
            for o in self.keys_r.get(k, []):
                self._dep(op, o, True, False)
            self.keys_w[k] = op
            self.keys_r[k] = []
        if dma is not None:
            dma.count += 16
            op.val = dma.count
            dma.cur.append(op)
        for d in op.deps:
            d.sig = True
        self.streams[eng].append(op)
        self.nops += 1
        return op

    def schedule(self):
        import heapq
        allops = []
        for e in self.ENGS:
            allops.extend(self.streams[e])
        allops.sort(key=lambda o: o.idx)
        for o in allops:
            o.succ = []
            o.fin = None
        for o in allops:
            ds = set(id(d) for d in o.deps) | set(id(d) for d in o.odeps)
            o.npend = 0
            seen = set()
            for d in list(o.deps) + list(o.odeps):
                if id(d) in seen:
                    continue
                seen.add(id(d))
                d.succ.append(o)
                o.npend += 1
        LAT = 150.0
        for o in reversed(allops):
            b = 0.0
            for s_ in o.succ:
                if s_.bl + LAT > b:
                    b = s_.bl + LAT
            o.bl = b + (o.cost if not o.isdma else o.cost + 120.0)
        free = {e: 0.0 for e in self.ENGS}
        waiting = {e: [] for e in self.ENGS}
        avail = {e: [] for e in self.ENGS}
        order = {e: [] for e in self.ENGS}

        def push(o):
            est = 0.0
            for d in o.deps:
                f = d.fin + LAT
                if f > est:
                    est = f
            for d in o.odeps:
                f = d.fin - d.cost if not d.isdma else d.est
                if f > est:
                    est = f
            o.est = est
            heapq.heappush(waiting[o.eng], (est, o.idx, o))
        for o in allops:
            if o.npend == 0:
                push(o)
        remaining = len(allops)
        while remaining:
            best = None
            for e in self.ENGS:
                w_, a_ = waiting[e], avail[e]
                while w_ and w_[0][0] <= free[e]:
                    est, idx, o = heapq.heappop(w_)
                    heapq.heappush(a_, (-o.bl if CRITPATH else idx, idx, o))
                if a_:
                    cand = (free[e], a_[0][1], e, True)
                elif w_:
                    cand = (w_[0][0], w_[0][1], e, False)
                else:
                    continue
                if best is None or cand < best:
                    best = cand
            t, idx, e, isav = best
            if isav:
                _, idx, o = heapq.heappop(avail[e])
            else:
                est, idx, o = heapq.heappop(waiting[e])
            start = max(free[e], o.est)
            if o.isdma:
                issue = 1500.0 if e == "pool" else 120.0
                free[e] = start + issue
                o.est = start
                o.fin = start + issue + o.cost
            else:
                free[e] = start + o.cost
                o.fin = free[e]
            order[e].append(o)
            remaining -= 1
            for s_ in o.succ:
                s_.npend -= 1
                if s_.npend == 0:
                    push(s_)
        for e in self.ENGS:
            assert len(order[e]) == len(self.streams[e])
            self.streams[e] = order[e]
        self.makespan = max(free.values())

    def emit(self, final_ops):
        nc = self.nc
        for e in self.ENGS:
            cnt = 0
            for op in self.streams[e]:
                if not op.isdma and op.sig:
                    cnt += 1
                    op.val = cnt
        import contextlib
        with contextlib.ExitStack() as es:
            esem = {e: es.enter_context(nc.semaphore("c_" + e)) for e in self.ENGS}
            for g in self.groups:
                g.sem = es.enter_context(nc.semaphore("d_" + g.name))
            block = es.enter_context(nc.Block())

            def run(e, eng):
                waited = {}
                for op in self.streams[e]:
                    need = {}
                    for d in op.deps:
                        if d.isdma:
                            sem = d.grp.sem
                            v = d.grp.count if d.grp.wait_total else d.val
                        else:
                            sem = esem[d.eng]
                            v = d.val
                        k = id(sem)
                        if waited.get(k, 0) >= v:
                            continue
                        if k not in need or need[k][1] < v:
                            need[k] = (sem, v)
                    items = list(need.items())
                    embed = None
                    if EMBED_WAIT and items and not op.isdma:
                        embed = items.pop()
                    for k, (sem, v) in items:
                        eng.wait_ge(sem, v)
                        waited[k] = v
                    ins = op.fn(eng)
                    if embed is not None:
                        k, (sem, v) = embed
                        ins._wait_ge(sem, v)
                        waited[k] = v
                    if op.isdma:
                        ins.then_inc(op.grp.sem, 16)
                    elif op.sig:
                        ins.then_inc(esem[e], 1)
                if e == "sp":
                    for d in final_ops:
                        sem = d.grp.sem
                        eng.wait_ge(sem, d.grp.count)

            @block.tensor
            def _(t):
                run("pe", t)

            @block.scalar
            def _(s):
                run("act", s)

            @block.vector
            def _(v):
                run("dve", v)

            @block.gpsimd
            def _(p):
                run("pool", p)

            @block.sync
            def _(s):
                run("sp", s)


class Arena:
    def __init__(self, nc, nbytes):
        self.t = nc.alloc_sbuf_tensor("arena", [128, nbytes], U8)
        self.nbytes = nbytes
        self.top = 0

    def at(self, off, shape, dt):
        n = 1
        for x in shape:
            n *= x
        nb = n * ISZ[dt]
        assert off % 4 == 0 and off + nb <= self.nbytes, (off, nb, self.nbytes)
        ap = self.t[:, off:off + nb].bitcast(dt)
        if len(shape) == 2:
            ap = ap.rearrange("p (a b) -> p a b", a=shape[0])
        elif len(shape) == 3:
            ap = ap.rearrange("p (a b c) -> p a b c", a=shape[0], b=shape[1])
        elif len(shape) == 4:
            ap = ap.rearrange("p (a b c d) -> p a b c d", a=shape[0], b=shape[1], c=shape[2])
        return ap

    def alloc(self, shape, dt):
        n = 1
        for x in shape:
            n *= x
        nb = (n * ISZ[dt] + 63) // 64 * 64
        ap = self.at(self.top, shape, dt)
        self.top += nb
        return ap


C_IDENT, C_TRIN, C_TRII, C_TRIR, C_ONES, C_MN1, C_MN0, C_PROT, C_LVL = 0, 1, 2, 3, 4, 5, 6, 7, 8
NCONST = 15
BIGNEG = -30000.0


def make_consts():
    s = np.arange(128)[:, None]
    t = np.arange(128)[None, :]
    c = np.zeros((128, NCONST, 128), np.float32)
    c[:, C_IDENT] = (s == t)
    c[:, C_TRIN] = np.where(s <= t, -1.0 / 16.0, 0.0)
    c[:, C_TRII] = (s <= t)
    c[:, C_TRIR] = (s > t)
    c[:, C_ONES] = 1.0
    c[:, C_MN1] = np.where(s < t, 0.0, BIGNEG)
    c[:, C_MN0] = np.where(s <= t, 0.0, BIGNEG)
    pr = np.zeros((128, 128), np.float32)
    for base in (0, 64):
        for d in range(8):
            pr[base + d + 8, base + d] = -1.0
            pr[base + d, base + d + 8] = 1.0
    c[:, C_PROT] = pr
    for j in range(1, 8):
        b = 1 << (j - 1)
        m = ((s // (2 * b)) == (t // (2 * b))) & (((s // b) % 2) == 0) & (((t // b) % 2) == 1)
        c[:, C_LVL + j - 1] = m
    cv = np.zeros((128, 8), np.float32)
    half = 8
    invf = (500000.0 ** (-np.arange(half, dtype=np.float32) / np.float32(half))).astype(np.float32)
    for base in (0, 64):
        for d in range(16):
            cv[base + d, 0] = invf[d % 8]
    return c, cv


def small_layout(L):
    o = {}
    off = 0

    def put(name, n):
        nonlocal off
        o[name] = (off, n)
        off += n
    put("c", 16)
    put("fng", 16)
    put("n1g", L * 16)
    put("n2g", L * 16)
    put("bmod", L * 96)
    put("gagain", L * 128)
    put("gcgain", L * 128)
    put("sinks", L * 16)
    put("convw", L * 48)
    put("alog", L * 4)
    put("dtb", L * 4)
    put("wgk", L * 256)
    put("cvec", 8)
    return o, off


def pack_small(inp, b, L):
    lay, n = small_layout(L)
    a = np.zeros((128, n), np.float32)

    def fm(v, nch):
        return np.ascontiguousarray(v.reshape(nch, 128).T)

    def setv(name, arr):
        o, k = lay[name]
        a[:, o:o + k] = arr.reshape(128, k)
    setv("c", fm(inp["c"][b], 16))
    setv("fng", fm(inp["final_norm_gain"], 16))
    setv("n1g", np.stack([fm(inp["norm1_gain"][l], 16) for l in range(L)], 1))
    setv("n2g", np.stack([fm(inp["norm2_gain"][l], 16) for l in range(L)], 1))
    setv("bmod", np.stack([fm(inp["b_mod"][l], 96) for l in range(L)], 1))
    setv("gagain", np.broadcast_to(inp["gla_norm_gain"][:L].reshape(1, L * 128), (128, L * 128)))
    setv("gcgain", np.broadcast_to(inp["gdn_norm_gain"][:L].reshape(1, L * 128), (128, L * 128)))
    setv("sinks", np.broadcast_to(inp["swa_sinks"][:L].reshape(1, L * 16), (128, L * 16)))
    cw = np.stack([np.stack([fm(inp["gdn_conv_w"][l, j], 12) for j in range(4)], 1) for l in range(L)], 1)
    setv("convw", cw)
    setv("alog", np.broadcast_to(inp["gdn_a_log"][:L].reshape(1, L * 4), (128, L * 4)))
    setv("dtb", np.broadcast_to(inp["gdn_dt_bias"][:L].reshape(1, L * 4), (128, L * 4)))
    wg = np.zeros((128, L, 256), np.float32)
    wg[:16] = np.transpose(inp["gla_w_gk"][:L], (1, 0, 2))
    wg[16] = inp["gla_b_gk"][:L]
    setv("wgk", wg)
    _, cv = make_consts()
    setv("cvec", cv)
    return a


IN_BLOCKS = [
    ("bq0", [(1552, 256)]), ("bq1", [(1808, 256)]), ("bq2", [(2064, 256)]), ("bq3", [(2320, 256)]),
    ("bk", [(2576, 64), (2576, 64), (2640, 64), (2640, 64)]), ("bv", [(2704, 128)]),
    ("aq", [(0, 256)]), ("ak", [(256, 256)]), ("alr", [(1536, 16)]),
    ("av0", [(512, 256)]), ("av1", [(768, 256)]), ("ag0", [(1024, 256)]), ("ag1", [(1280, 256)]),
    ("cq0", [(2832, 256)]), ("cq1", [(3088, 256)]), ("ck0", [(3344, 256)]), ("ck1", [(3600, 256)]),
    ("cv0", [(3856, 256)]), ("cv1", [(4112, 256)]), ("cz0", [(4368, 256)]), ("cz1", [(4624, 256)]),
    ("cbca", [(4880, 8)]),
]


def build(T, L, stop_after=None, dumps=()):
    nc = bass.Bass("TRN2", target_bir_lowering=False)
    S = Sched(nc)
    NT = T // TT
    lay, NS = small_layout(L)
    dumps = set(dumps)

    x_d = nc.dram_tensor("x", [T, D], F32, kind="ExternalInput").ap()
    pos_d = nc.dram_tensor("pos", [128, T], I32, kind="ExternalInput").ap()
    small_d = nc.dram_tensor("small", [128, NS], F32, kind="ExternalInput").ap()
    const_d = nc.dram_tensor("consts", [128, NCONST, 128], F32, kind="ExternalInput").ap()
    wmod_d = nc.dram_tensor("w_mod", [L, D, 6 * D], F32, kind="ExternalInput").ap()
    win_d = nc.dram_tensor("w_in", [L, D, INW], F32, kind="ExternalInput").ap()
    wout_d = nc.dram_tensor("w_out", [L, D, D], F32, kind="ExternalInput").ap()
    wg_d = nc.dram_tensor("w_gate", [L, D, DFF], F32, kind="ExternalInput").ap()
    wu_d = nc.dram_tensor("w_up", [L, D, DFF], F32, kind="ExternalInput").ap()
    wd_d = nc.dram_tensor("w_down", [L, DFF, D], F32, kind="ExternalInput").ap()
    out_d = nc.dram_tensor("out", [T, D], F32, kind="ExternalOutput").ap()

    in_off = {}
    o = 0
    for name, segs in IN_BLOCKS:
        w = sum(s[1] for s in segs)
        in_off[name] = (o, w)
        o += 128 * KC * w
    IN_ELEMS = o
    sc_in = [nc.dram_tensor("sc_in%d" % l, [IN_ELEMS], BF16, kind="Internal").ap() for l in range(L)]
    sc_out = [nc.dram_tensor("sc_out%d" % l, [8, 128, KC, 256], BF16, kind="Internal").ap() for l in range(L)]
    sc_g = [nc.dram_tensor("sc_g%d" % l, [22, 128, KC, 256], BF16, kind="Internal").ap() for l in range(L)]
    sc_u = [nc.dram_tensor("sc_u%d" % l, [22, 128, KC, 256], BF16, kind="Internal").ap() for l in range(L)]
    sc_d = [nc.dram_tensor("sc_d%d" % l, [32, 128, 22, 128], BF16, kind="Internal").ap() for l in range(L)]

    def in_view(l, name):
        o, w = in_off[name]
        return sc_in[l][o:o + 128 * KC * w].rearrange("(p k c) -> p k c", p=128, k=KC), w

    A = Arena(nc, 208000)
    consts = A.alloc([NCONST, 128], F32)
    cb16 = A.alloc([4, 128], BF16)
    small = A.alloc([NS], F32)
    modT = A.alloc([L, 6, 16], F32)
    scalA = A.alloc([L, 2, 16], F32)
    esink = A.alloc([L, 16], F32)
    aexp = A.alloc([L, 4], F32)
    xT = A.alloc([KC, TT], F32)
    hT = A.alloc([KC, TT], BF16)
    NWB = 3
    wbuf = [A.alloc([4096], BF16) for _ in range(NWB)]
    cosT = A.alloc([TT], F32)
    sinT = A.alloc([TT], F32)
    glaS = [A.alloc([4, 128], F32) for _ in range(L)]
    glaSb = [A.alloc([4, 128], BF16) for _ in range(L)]
    gdnS = [A.alloc([4, 128], F32) for _ in range(L)]
    gdnSb = [A.alloc([4, 128], BF16) for _ in range(L)]
    kcar = [A.alloc([2, 128], BF16) for _ in range(L)]
    vcar = [A.alloc([2, 66], BF16) for _ in range(L)]
    ccar = [A.alloc([12, 4], F32) for _ in range(L)]
    RBASE = A.top
    RSIZE = A.nbytes - RBASE

    def sm(name):
        o, n = lay[name]
        return small[:, o:o + n]

    ps = nc.alloc_psum_tensor("ps", [128, 4096], F32)

    def bank(b, n=512):
        return ps[:, b * 512:b * 512 + n]

    C = lambda i: consts[:, i, :]
    identb, onesb, maskCb, maskPb = cb16[:, 0, :], cb16[:, 1, :], cb16[:, 2, :], cb16[:, 3, :]

    def mm(out, lhsT, rhs, start=True, stop=True):
        n = rhs.shape[-1]
        c = max(n, 64) / 2.4 * (4.0 if rhs.dtype == F32 else 1.0) + 6.0
        return S.add("pe", lambda e: e.matmul(out, lhsT, rhs, start=start, stop=stop), r=[lhsT, rhs], w=[out], cost=c)

    def tr(out, in_, ident):
        c = 64.0 * (4.0 if in_.dtype == F32 else 1.0)
        return S.add("pe", lambda e: e.transpose(out, in_, ident), r=[in_, ident], w=[out], cost=c)

    def act(out, in_, func, bias=0.0, scale=1.0, accum=None):
        r = [in_]
        if not isinstance(bias, float):
            r.append(bias)
        if not isinstance(scale, float):
            r.append(scale)
        w = [out] + ([accum] if accum is not None else [])
        if accum is not None:
            return S.add("act", lambda e: e.activation(out, in_, func, bias=bias, scale=scale, accum_out=accum), r=r, w=w)
        return S.add("act", lambda e: e.activation(out, in_, func, bias=bias, scale=scale), r=r, w=w)

    def tt(eng, out, in0, in1, op):
        return S.add(eng, lambda e: e.tensor_tensor(out, in0, in1, op), r=[in0, in1], w=[out])

    def ts(eng, out, in0, s1, s2, op0, op1=None):
        r = [in0] + [s for s in (s1, s2) if s is not None and not isinstance(s, float)]
        if op1 is None:
            return S.add(eng, lambda e: e.tensor_scalar(out, in0, s1, None, op0), r=r, w=[out])
        return S.add(eng, lambda e: e.tensor_scalar(out, in0, s1, s2, op0, op1), r=r, w=[out])

    def stt(eng, out, in0, sc, in1, op0, op1):
        r = [in0, in1] + ([] if isinstance(sc, float) else [sc])
        return S.add(eng, lambda e: e.scalar_tensor_tensor(out, in0, sc, in1, op0, op1), r=r, w=[out])

    def cp(eng, out, in_):
        if eng == "act":
            return S.add("act", lambda e: e.copy(out, in_), r=[in_], w=[out])
        return S.add(eng, lambda e: e.tensor_copy(out, in_), r=[in_], w=[out])

    def memset(eng, out, v):
        return S.add(eng, lambda e: e.memset(out, v), w=[out])

    def dma(q, out, in_, grp, r=(), w=(), kr=(), kw=()):
        nb = ISZ[out.dtype]
        for d_ in out.shape:
            nb *= d_
        return S.add(q, lambda e: e.dma_start(out=out, in_=in_), r=r, w=w, kr=kr, kw=kw, dma=grp,
                     cost=2000.0 + nb / 150.0)
    dma_real = dma

    dump_ops = []
    g_dump = S.group("dump")

    def dump(name, ap, shape):
        if name not in dumps:
            return
        dd = nc.dram_tensor("dbg_" + name, [ap.shape[0]] + list(shape), ap.dtype, kind="ExternalOutput").ap()
        dump_ops.append(dma("sp", dd, ap, g_dump, r=[ap]))

    g_pro = S.group("pro")
    g_pos = S.group("pos")
    dma("sp", consts, const_d, g_pro, w=[consts])
    dma("sp", small, small_d, g_pro, w=[small])
    g_pro.close()
    for i, ci in enumerate((C_IDENT, C_ONES, C_TRII, C_TRIR)):
        cp("dve", cb16[:, i, :], C(ci))
    for l in range(L):
        memset("pool", glaS[l], 0.0)
        memset("pool", glaSb[l], 0.0)
        memset("pool", gdnS[l], 0.0)
        memset("pool", gdnSb[l], 0.0)
        memset("pool", kcar[l], 0.0)
        memset("pool", vcar[l], 0.0)
        memset("pool", ccar[l], 0.0)
    act(esink, sm("sinks").rearrange("p (l h) -> p l h", l=L), AF.Exp)
    act(aexp, sm("alog").rearrange("p (l h) -> p l h", l=L), AF.Exp)
    ts("dve", aexp, aexp, -1.0, None, ALU.mult)
    cact = A.at(RBASE, [16], F32)
    act(cact, sm("c"), AF.Silu)

    g_cast = {}

    pending_casts = {}

    def cast_layer(l, defer=False):
        lst = []

        def dma(q, out, in_, grp, **kw):
            lst.append(lambda: dma_real(q, out, in_, grp, **kw))
        for nm in ("in", "out", "g", "u", "d"):
            g_cast[(l, nm)] = S.group("cast_%s%d" % (nm, l), wait_total=True)
        gi = g_cast[(l, "in")]
        for name, segs in IN_BLOCKS:
            v, w = in_view(l, name)
            c0 = 0
            for (sc, sw) in segs:
                for k0 in range(0, KC, 4):
                    src = win_d[l, k0 * 128:(k0 + 4) * 128, sc:sc + sw].rearrange("(k p) c -> p k c", p=128)
                    dma("pool", v[:, k0:k0 + 4, c0:c0 + sw], src, gi, kw=[("sc_in", l)])
                c0 += sw
        for (nm, scr, wd, nb) in (("out", sc_out, wout_d, 8), ("g", sc_g, wg_d, 22), ("u", sc_u, wu_d, 22)):
            gg = g_cast[(l, nm)]
            for b in range(nb):
                for k0 in range(0, KC, 4):
                    src = wd[l, k0 * 128:(k0 + 4) * 128, b * 256:(b + 1) * 256].rearrange("(k p) c -> p k c", p=128)
                    dma("pool", scr[l][b, :, k0:k0 + 4, :], src, gg, kw=[("sc_" + nm, l)])
        gg = g_cast[(l, "d")]
        for m in range(16):
            for kh in range(2):
                for k0 in (0, 11):
                    kk = kh * 22 + k0
                    src = wd_d[l, kk * 128:(kk + 11) * 128, m * 128:(m + 1) * 128].rearrange("(k p) c -> p k c", p=128)
                    dma("pool", sc_d[l][m * 2 + kh, :, k0:k0 + 11, :], src, gg, kw=[("sc_d", l)])
        if defer:
            pending_casts[l] = lst
        else:
            for f in lst:
                f()

    def emit_casts(l, frac_idx, nfrac):
        lst = pending_casts.get(l)
        if not lst:
            return
        n = len(lst)
        for f in lst[n * frac_idx // nfrac:n * (frac_idx + 1) // nfrac]:
            f()

    cast_layer(0)

    wm_stage = [A.at(RBASE + 64 + i * 32768, [KC, 512], F32) for i in range(2)]
    macc = A.at(RBASE + 64 + 65536, [512], F32)
    g_wm = [S.group("wm%d" % i) for i in range(2)]
    npiece = 0
    for l in range(L):
        for pc in range(24):
            st = wm_stage[npiece % 2]
            src = wmod_d[l, :, pc * 512:(pc + 1) * 512].rearrange("(k p) c -> p k c", p=128)
            for k0 in range(0, KC, 4):
                dma("sp" if npiece % 2 == 0 else "act", st[:, k0:k0 + 4, :], src[:, k0:k0 + 4, :], g_wm[npiece % 2],
                    w=[st[:, k0:k0 + 4, :]])
            g_wm[npiece % 2].close()
            eng = "dve"
            ts(eng, macc, st[:, 0, :], cact[:, 0:1], None, ALU.mult)
            for kc in range(1, KC):
                stt(eng, macc, st[:, kc, :], cact[:, kc:kc + 1], macc, ALU.mult, ALU.add)
            pb = bank(4 + (npiece % 2), 4)
            for j in range(4):
                mm(pb[:, j:j + 1], macc[:, j * 128:(j + 1) * 128], consts[:, C_ONES, 0:1])
            jj = pc * 4
            dst = modT[:, l, :, :].rearrange("p a b -> p (a b)")[:, jj:jj + 4]
            bm = sm("bmod")[:, l * 96 + jj:l * 96 + jj + 4]
            tt("dve", dst, pb, bm, ALU.add)
            npiece += 1
        if l + 1 < L:
            cast_layer(l + 1, defer=True)
    for l in range(L):
        for wch, (scl, gn) in enumerate(((1, "n1g"), (4, "n2g"))):
            g = sm(gn)[:, l * 16:(l + 1) * 16]
            stt("dve", scalA[:, l, wch, :], modT[:, l, scl, :], 1.0, g, ALU.add, ALU.mult)
            ts("dve", scalA[:, l, wch, :], scalA[:, l, wch, :], float(np.sqrt(D)), None, ALU.mult)
    dump("modT", modT, [L, 6, 16])
    dump("scalA", scalA, [L, 2, 16])

    g_w = [S.group("w%d" % i) for i in range(NWB)]
    wctr = [0]

    def wload(src_ap, shape, keyr):
        i = wctr[0] % NWB
        wctr[0] += 1
        a, b = shape
        dst = wbuf[i][:, 0:a * b].rearrange("p (a b) -> p a b", a=a)
        dma("act" if i == 1 else "sp", dst, src_ap, g_w[i], w=[dst], kr=[keyr])
        return dst

    def fm_norm(l, wch, dst_bf, tmpbase):
        sq = A.at(tmpbase, [4, TT], BF16)
        rb = A.at(tmpbase + 4096, [TT], F32)
        tmp = A.at(tmpbase + 6144, [2, TT], F32)
        pb = bank(0)
        for k4 in range(4):
            act(sq, xT[:, k4 * 4:(k4 + 1) * 4, :], AF.Square)
            for j in range(4):
                kc = k4 * 4 + j
                mm(pb, onesb, sq[:, j, :], start=(kc == 0), stop=(kc == KC - 1))
        act(rb, pb, AF.Ln, bias=float(D * EPS), scale=1.0)
        act(rb, rb, AF.Exp, scale=-0.5)
        for kc in range(KC):
            t_ = tmp[:, kc % 2, :]
            tt("dve" if kc % 2 == 0 else "pool", t_, xT[:, kc, :], rb, ALU.mult)
            if l is None:
                g = sm("fng")[:, kc:kc + 1]
                ts("dve", dst_bf[:, kc, :], t_, g, float(np.sqrt(D)), ALU.mult, ALU.mult)
            else:
                act(dst_bf[:, kc, :], t_, AF.Identity, bias=modT[:, l, 3 * wch, kc:kc + 1],
                    scale=scalA[:, l, wch, kc:kc + 1])

    g_xin = [S.group("xin%d" % i) for i in range(2)]
    g_out = [S.group("xout%d" % i) for i in range(2)]
    out_ops = []
    R = RBASE

    def tile_head(it):
        t0 = it * TT
        for b in range(NBLK):
            stg = A.at(R + (b % 2) * 8192, [D], F32)
            dma("sp", stg, x_d[t0 + b * 128:t0 + (b + 1) * 128, :], g_xin[b % 2], w=[stg])
            for k4 in range(4):
                pb = bank(k4 % 4)
                for j in range(4):
                    kc = k4 * 4 + j
                    tr(pb[:, j * 128:(j + 1) * 128], stg[:, kc * 128:(kc + 1) * 128], C(C_IDENT))
                cp("dve" if k4 % 2 == 0 else "act", xT[:, k4 * 4:(k4 + 1) * 4, b * 128:(b + 1) * 128],
                   pb.rearrange("p (a b) -> p a b", a=4))
        posi = A.at(R + 16384, [TT], I32)
        posf = A.at(R + 16384 + 2048, [TT], F32)
        ang = A.at(R + 16384 + 4096, [TT], F32)
        kk_i = A.at(R + 16384 + 6144, [TT], I32)
        kk_f = A.at(R + 16384 + 8192, [TT], F32)
        mtmp = A.at(R + 16384 + 10240, [TT], F32)
        dma("sp", posi, pos_d[:, t0:t0 + TT], g_pos, w=[posi])
        cp("dve", posf, posi)
        invf = sm("cvec")[:, 0:1]
        for (dst, shift) in ((sinT, 0.0), (cosT, float(np.pi / 2))):
            ts("dve", ang, posf, invf, shift, ALU.mult, ALU.add)
            ts("dve", mtmp, ang, float(1.0 / (2 * np.pi)), None, ALU.mult)
            cp("dve", kk_i, mtmp)
            cp("dve", kk_f, kk_i)
            stt("dve", ang, kk_f, float(-2 * np.pi), ang, ALU.mult, ALU.add)
            ts("dve", mtmp, ang, float(np.pi), float(2 * np.pi), ALU.is_gt, ALU.mult)
            tt("dve", ang, ang, mtmp, ALU.subtract)
            ts("dve", mtmp, ang, float(-np.pi), float(2 * np.pi), ALU.is_lt, ALU.mult)
            tt("dve", ang, ang, mtmp, ALU.add)
            act(dst, ang, AF.Sin)
        dump("cos%d" % it, cosT, [TT])
        dump("sin%d" % it, sinT, [TT])
        dump("xT%d" % it, xT, [KC, TT])

    bigctr = [0]

    bigmod = [4]

    def bigbank():
        b = bigctr[0] % bigmod[0]
        bigctr[0] += 1
        return bank(b)

    def proj_fm(wb, c0, m, consume, pbase=0, prows=128):
        pb = bigbank()[pbase:pbase + m, :]
        for kc in range(KC):
            mm(pb, wb[:, kc, c0:c0 + m], hT[:, kc, :], start=(kc == 0), stop=(kc == KC - 1))
        consume(pb)

    def proj_tm(wb, b, c0, n, consume):
        pb = bigbank()[:, 0:n]
        for kc in range(KC):
            mm(pb, hT[:, kc, b * 128:(b + 1) * 128], wb[:, kc, c0:c0 + n], start=(kc == 0), stop=(kc == KC - 1))
        consume(pb)

    MIXT_OFF = 0
    MIX_OFF = 16384
    MB_OFF = 24576

    def mix_to_T(mix, b, ncols, chunk0):
        mixT = A.at(R + MIXT_OFF, [KC, TT], BF16)
        for c4 in range(0, ncols // 128, 4):
            pbb = bigbank().bitcast(BF16)[:, 0:512]
            for j in range(4):
                tr(pbb[:, j * 128:(j + 1) * 128], mix[:, b, (c4 + j) * 128:(c4 + j + 1) * 128], identb)
            cp("dve", mixT[:, chunk0 + c4:chunk0 + c4 + 4, b * 128:(b + 1) * 128],
               pbb.rearrange("p (a b) -> p a b", a=4))

    def swa(l, it):
        MB = R + MB_OFF
        mix = A.at(R + MIX_OFF, [NBLK, 1024], BF16)
        NQB = 4
        q32 = A.at(MB, [NQB, TT], F32)
        qT = A.at(MB + 8192, [16, TT], BF16)
        kT = A.at(MB + 24576, [2, 640], BF16)
        vaug = A.at(MB + 27136, [5, 2, 66], BF16)
        rt = A.at(MB + 28480, [NQB, 2, TT], F32)
        pT = [A.at(MB + 28480 + i * 4096, [2, 8, 128], BF16) for i in range(2)]
        den = A.at(MB + 44864, [2, 8], F32)
        rcp = A.at(MB + 44864 + 64, [2, 8], F32)
        assert MB + 44864 + 128 <= A.nbytes
        cp("pool", kT[:, :, 0:128], kcar[l])
        cp("pool", vaug[:, 0, :, :], vcar[l])
        memset("pool", vaug[:, 1:5, :, 64:65], 1.0)
        qTe = qT.rearrange("p (c e) t -> p c e t", e=2)
        memset("pool", qTe[0:64, :, 1, :], 0.0)
        memset("pool", qTe[64:128, :, 0, :], 0.0)
        ci = [0]
        for bi, name in enumerate(["bq0", "bq1", "bq2", "bq3", "bk"]):
            v, w = in_view(l, name)
            wb = wload(v, (KC, w), ("sc_in", l))
            first_pbs = None
            if bi == 0:
                first_pbs = [bigbank() for _ in range(2)]
                for kc in range(KC):
                    for c in range(2):
                        mm(first_pbs[c], wb[:, kc, c * 128:(c + 1) * 128], hT[:, kc, :], start=(kc == 0), stop=(kc == KC - 1))
            for c in range(2):
                def consume(pb, bi=bi, c=c):
                    k_ = ci[0] % NQB
                    q_ = q32[:, k_, :]
                    ci[0] += 1
                    cp("act", q_, pb)
                    rp = bank(4 + k_)
                    mm(rp, C(C_PROT), q_)
                    tt("pool", rt[:, k_, 0, :], q_, cosT, ALU.mult)
                    tt("dve", rt[:, k_, 1, :], rp, sinT, ALU.mult)
                    if bi < 4:
                        ch = bi * 2 + c
                        tt("pool", qT[0:64, 2 * ch, :], rt[0:64, k_, 0, :], rt[0:64, k_, 1, :], ALU.add)
                        tt("pool", qT[64:128, 2 * ch + 1, :], rt[64:128, k_, 0, :], rt[64:128, k_, 1, :], ALU.add)
                    else:
                        tt("pool", kT[:, c, 128:640], rt[:, k_, 0, :], rt[:, k_, 1, :], ALU.add)
                if first_pbs is not None:
                    consume(first_pbs[c])
                else:
                    proj_fm(wb, c * 128, 128, consume)
        v, w = in_view(l, "bv")
        wb = wload(v, (KC, 128), ("sc_in", l))
        for b in range(NBLK):
            proj_tm(wb, b, 0, 128, lambda pb, b=b: cp("act", vaug[:, 1 + b, :, 0:64],
                                                      pb.rearrange("p (g d) -> p g d", g=2)))
        dump("qT_l%d_t%d" % (l, it), qT, [16, TT])
        dump("kT_l%d_t%d" % (l, it), kT, [2, 640])
        dump("vaug_l%d_t%d" % (l, it), vaug, [5, 2, 66])
        if stop_after == "swa_proj":
            return
        n = 0
        for b in range(NBLK):
            for g in range(2):
                pt = pT[n % 2]
                for jb in range(2):
                    sp_ = ps[:, 2048 + jb * 1024:2048 + (jb + 1) * 1024]
                    kcols = kT[:, g, (b + jb) * 128:(b + jb + 1) * 128]
                    for hh in range(8):
                        mm(sp_[:, hh * 128:(hh + 1) * 128], kcols,
                           qT[:, g * 8 + hh, b * 128:(b + 1) * 128])
                    if stop_after == "swa_c1":
                        return
                    for hb in range(2):
                        act(pt[:, jb, hb * 4:(hb + 1) * 4, :],
                            sp_[:, hb * 512:(hb + 1) * 512].rearrange("p (h q) -> p h q", h=4), AF.Exp, scale=0.125)
                    if stop_after == "swa_c2":
                        return
                    mk = maskPb if jb == 0 else maskCb
                    tt("pool", pt[:, jb, :, :], pt[:, jb, :, :], mk.unsqueeze(1).to_broadcast([128, 8, 128]), ALU.mult)
                if stop_after == "swa_c3":
                    return
                ob = ps[:, (n % 2) * 1024:(n % 2) * 1024 + 1024].rearrange("p (h c) -> p h c", h=8)
                for hh in range(8):
                    for jb in range(2):
                        mm(ob[:, hh, 0:65], pt[:, jb, hh, :], vaug[:, b + jb, g, 0:65], start=(jb == 0), stop=(jb == 1))
                if stop_after == "swa_c4":
                    return
                dn = den[:, n % 2, :]
                rc = rcp[:, n % 2, :]
                tt("dve", dn, ob[:, :, 64], esink[:, l, g * 8:(g + 1) * 8], ALU.add)
                S.add("dve", lambda e, rc=rc, dn=dn: e.reciprocal(rc, dn), r=[dn], w=[rc])
                tt("dve", mix[:, b, g * 512:(g + 1) * 512].rearrange("p (h d) -> p h d", h=8), ob[:, :, 0:64],
                   rc.unsqueeze(2).to_broadcast([128, 8, 64]), ALU.mult)
                n += 1
                if stop_after == "swa_c5":
                    return
            mix_to_T(mix, b, 1024, 4)
            if stop_after == "swa_c6":
                return
        cp("pool", kcar[l], kT[:, :, 512:640])
        cp("pool", vcar[l], vaug[:, 4, :, :])
        dump("mixB_l%d_t%d" % (l, it), mix, [NBLK, 1024])

    def gla(l, it):
        MB = R + MB_OFF
        mix = A.at(R + MIX_OFF, [NBLK, 512], BF16)
        aqT = A.at(MB, [4, TT], F32)
        akT = A.at(MB + 8192, [4, TT], F32)
        alrT = A.at(MB + 16384, [TT], F32)
        av = A.at(MB + 18432, [NBLK, 512], BF16)
        sga = A.at(MB + 22528, [NBLK, 512], F32)
        e_ = A.at(MB + 30720, [256], F32)
        l_ = A.at(MB + 31744, [256], F32)
        ebT = A.at(MB + 32768, [4, 128], F32)
        enbT = A.at(MB + 34816, [4, 128], F32)
        qtT = A.at(MB + 37888, [4, 128], BF16)
        ktT = A.at(MB + 38912, [4, 128], BF16)
        kt = A.at(MB + 39936, [4, 64], BF16)
        AmT = A.at(MB + 40448, [4, 128], BF16)
        osq = A.at(MB + 41472, [512], F32)
        ss = A.at(MB + 43520, [4], F32)
        rstd = A.at(MB + 43584, [4], F32)
        og = A.at(MB + 43648, [512], F32)
        gain = sm("gagain")[:, l * 128:(l + 1) * 128]
        wgk = sm("wgk")[0:17, l * 256:(l + 1) * 256]
        memset("pool", alrT[0:32, :], 1.0)
        for name, dstT in (("aq", aqT), ("ak", akT)):
            v, w = in_view(l, name)
            wb = wload(v, (KC, 256), ("sc_in", l))
            for h in range(4):
                proj_fm(wb, h * 64, 64, lambda pb, h=h, dstT=dstT: cp("act", dstT[0:64, h, :], pb))
        v, w = in_view(l, "alr")
        wb = wload(v, (KC, 16), ("sc_in", l))
        proj_fm(wb, 0, 16, lambda pb: cp("act", alrT[0:16, :], pb))
        for half in range(2):
            v, w = in_view(l, "av%d" % half)
            wb = wload(v, (KC, 256), ("sc_in", l))
            for b in range(NBLK):
                proj_tm(wb, b, 0, 256, lambda pb, b=b, half=half: cp("act", av[:, b, half * 256:(half + 1) * 256], pb))
        for half in range(2):
            v, w = in_view(l, "ag%d" % half)
            wb = wload(v, (KC, 256), ("sc_in", l))
            for b in range(NBLK):
                def consume(pb, b=b, half=half):
                    d_ = sga[:, b, half * 256:(half + 1) * 256]
                    act(d_, pb, AF.Silu)
                    d3 = d_.rearrange("p (h d) -> p h d", h=2)
                    stt("dve", d3, d3, float(np.sqrt(128.0)), gain.unsqueeze(1).to_broadcast([128, 2, 128]),
                        ALU.mult, ALU.mult)
                proj_tm(wb, b, 0, 256, consume)
        Sl = glaS[l]
        Sb = glaSb[l]
        for b in range(NBLK):
            bc = slice(b * 128, (b + 1) * 128)
            pp = bank(4)[:, 0:256]
            mm(pp, alrT[0:17, bc], wgk)
            act(e_, pp, AF.Exp, scale=-1.0)
            act(l_, e_, AF.Ln, bias=1.0)
            pbT = bank(5)[0:64, :]
            for h in range(4):
                mm(pbT[:, h * 128:(h + 1) * 128], l_[:, h * 64:(h + 1) * 64], C(C_TRIN))
            pb3 = pbT.rearrange("p (h t) -> p h t", h=4)
            act(ebT[0:64], pb3, AF.Exp)
            act(enbT[0:64], pb3, AF.Exp, scale=-1.0)
            stt("dve", qtT[0:64], aqT[0:64, :, bc], 0.125, ebT[0:64], ALU.mult, ALU.mult)
            tt("pool", ktT[0:64], akT[0:64, :, bc], enbT[0:64], ALU.mult)
            pk = bank(6).bitcast(BF16)
            for h in range(4):
                tr(pk[:, h * 64:(h + 1) * 64], ktT[0:64, h, :], identb[0:64, 0:64])
            cp("dve", kt, pk[:, 0:256].rearrange("p (h d) -> p h d", h=4))
            pa = bank(7)
            for h in range(4):
                mm(pa[:, h * 128:(h + 1) * 128], ktT[0:64, h, :], qtT[0:64, h, :])
            tt("dve", AmT, pa.rearrange("p (h t) -> p h t", h=4), C(C_TRII).unsqueeze(1).to_broadcast([128, 4, 128]),
               ALU.mult)
            po = bank(4)
            for h in range(4):
                mm(po[:, h * 128:(h + 1) * 128], AmT[:, h, :], av[:, b, h * 128:(h + 1) * 128], start=True, stop=False)
                mm(po[:, h * 128:(h + 1) * 128], qtT[0:64, h, :], Sb[0:64, h, :], start=False, stop=True)
            act(osq, po, AF.Square)
            S.add("dve", lambda e: e.tensor_reduce(ss, osq.rearrange("p (h d) -> p h d", h=4), AX.X, ALU.add),
                  r=[osq], w=[ss])
            act(rstd, ss, AF.Ln, bias=float(128 * EPS))
            act(rstd, rstd, AF.Exp, scale=-0.5)
            tt("dve", og.rearrange("p (h d) -> p h d", h=4), po.rearrange("p (h d) -> p h d", h=4),
               rstd.unsqueeze(2).to_broadcast([128, 4, 128]), ALU.mult)
            tt("pool", mix[:, b, :], og, sga[:, b, :], ALU.mult)
            pP = bank(5)[0:64, :]
            for h in range(4):
                mm(pP[:, h * 128:(h + 1) * 128], kt[:, h, :], av[:, b, h * 128:(h + 1) * 128])
            tt("dve", Sl[0:64], Sl[0:64], pP.rearrange("p (h d) -> p h d", h=4), ALU.add)
            tt("dve", Sl[0:64], Sl[0:64], ebT[0:64, :, 127:128].to_broadcast([64, 4, 128]), ALU.mult)
            cp("act", Sb[0:64], Sl[0:64])
            mix_to_T(mix, b, 512, 0)
        dump("mixA_l%d_t%d" % (l, it), mix, [NBLK, 512])

    def gdn(l, it):
        MB = R + MB_OFF
        mix = A.at(R + MIX_OFF, [NBLK, 512], BF16)
        cin = A.at(MB, [2, 516], F32)
        cacc = A.at(MB + 4128, [2, TT], F32)
        sqb = A.at(MB + 8224, [2, TT], BF16)
        rn = A.at(MB + 10272, [2, TT], F32)
        hs0_ = MB + 30880 + 640 + 4096
        cin2 = A.at(hs0_, [2, 516], F32)
        cacc2 = A.at(hs0_ + 4128, [2, TT], F32)
        sqb2 = A.at(hs0_ + 8224, [2, TT], BF16)
        rn2 = A.at(hs0_ + 10272, [2, TT], F32)
        NCB = 4
        cinb = [cin[:, 0, :], cin[:, 1, :], cin2[:, 0, :], cin2[:, 1, :]]
        caccb = [cacc[:, 0, :], cacc[:, 1, :], cacc2[:, 0, :], cacc2[:, 1, :]]
        sqbb = [sqb[:, 0, :], sqb[:, 1, :], sqb2[:, 0, :], sqb2[:, 1, :]]
        rnb = [rn[:, 0, :], rn[:, 1, :], rn2[:, 0, :], rn2[:, 1, :]]
        qnT = A.at(MB + 14368, [4, TT], BF16)
        knT = A.at(MB + 18464, [4, TT], BF16)
        cvT = A.at(MB + 22560, [4, TT], BF16)
        sgz = A.at(MB + 26656, [NBLK, 512], BF16)
        bca = A.at(MB + 30752, [NBLK, 8], F32)
        sc0 = MB + 30880
        e1 = A.at(sc0, [4], F32)
        l1 = A.at(sc0 + 64, [4], F32)
        beta = A.at(sc0 + 128, [4], F32)
        t2 = A.at(sc0 + 192, [4], F32)
        l2 = A.at(sc0 + 256, [4], F32)
        g_ = A.at(sc0 + 320, [4], F32)
        eg3 = A.at(sc0 + 384, [3, 4], F32)
        ngc = A.at(sc0 + 448, [4], F32)
        bgk = A.at(sc0 + 512, [4], F32)
        nl1 = A.at(sc0 + 576, [4], F32)
        Gbc = A.at(sc0 + 640, [4, 128], F32)
        LBbc = A.at(sc0 + 640 + 2048, [4, 128], F32)
        hs0 = sc0 + 640 + 4096
        LT = A.at(MB, [4, 128], F32)
        Mj = A.at(MB + 2048, [2, 4, 128], F32)
        Tm = A.at(MB + 6144, [4, 128], F32)
        Ym = A.at(MB + 8192, [4, 128], F32)
        Wm = A.at(MB + 10240, [4, 128], F32)
        Yb = A.at(MB + 12288, [4, 128], BF16)
        attnT = A.at(MB + 13312, [4, 128], BF16)
        EE = A.at(hs0, [4, 2, 128], F32)
        EG = A.at(hs0 + 4096, [4, 128], F32)
        qgT = A.at(hs0 + 6144, [4, 128], BF16)
        vb = A.at(hs0 + 7168, [4, 128], BF16)
        kbg = A.at(hs0 + 8192, [4, 128], BF16)
        kdec = A.at(hs0 + 9216, [4, 128], BF16)
        wT = A.at(hs0 + 10240, [4, 128], BF16)
        vnew = A.at(hs0 + 11264, [4, 128], BF16)
        ub = A.at(hs0 + 12288, [4, 128], F32)
        osq = A.at(hs0 + 14336, [512], F32)
        og = A.at(hs0 + 16384, [512], F32)
        ss = A.at(hs0 + 18432, [4], F32)
        rstd = A.at(hs0 + 18496, [4], F32)
        assert hs0 + 18560 <= A.nbytes, (hs0 + 18560, A.nbytes)
        gain = sm("gcgain")[:, l * 128:(l + 1) * 128]
        convw = sm("convw")[:, l * 48:(l + 1) * 48].rearrange("p (j c) -> p j c", j=4)
        n = 0
        for bi, name in enumerate(["cq0", "cq1", "ck0", "ck1", "cv0", "cv1"]):
            v, w = in_view(l, name)
            wb = wload(v, (KC, 256), ("sc_in", l))
            for c in range(2):
                ch = bi * 2 + c

                def consume(pb, ch=ch):
                    nonlocal n
                    ci_ = cinb[n % NCB]
                    ca_ = caccb[n % NCB]
                    cp("pool", ci_[:, 0:3], ccar[l][:, ch, 0:3])
                    cp("act", ci_[:, 3:515], pb)
                    cp("pool", ccar[l][:, ch, 0:3], ci_[:, 512:515])
                    ts("dve", ca_, ci_[:, 0:512], convw[:, 0, ch:ch + 1], None, ALU.mult)
                    for j in range(1, 4):
                        stt("dve", ca_, ci_[:, j:j + 512], convw[:, j, ch:ch + 1], ca_, ALU.mult, ALU.add)
                    act(ca_, ca_, AF.Silu)
                    if ch < 8:
                        sq_ = sqbb[n % NCB]
                        rn_ = rnb[n % NCB]
                        act(sq_, ca_, AF.Square)
                        pbs = bank(4 + n % NCB)
                        mm(pbs, onesb, sq_)
                        act(rn_, pbs, AF.Ln, bias=float(EPS))
                        act(rn_, rn_, AF.Exp, scale=-0.5)
                        if ch < 4:
                            stt("dve", qnT[:, ch, :], ca_, float(128.0 ** -0.5), rn_, ALU.mult, ALU.mult)
                        else:
                            tt("dve", knT[:, ch - 4, :], ca_, rn_, ALU.mult)
                    else:
                        cp("dve", cvT[:, ch - 8, :], ca_)
                    n += 1
                proj_fm(wb, c * 128, 128, consume)
        for half in range(2):
            v, w = in_view(l, "cz%d" % half)
            wb = wload(v, (KC, 256), ("sc_in", l))
            for b in range(NBLK):
                def consume(pb, b=b, half=half):
                    tmpz = caccb[(b * 2 + half) % NCB][:, 0:256]
                    act(tmpz, pb, AF.Silu)
                    stt("dve", sgz[:, b, half * 256:(half + 1) * 256].rearrange("p (h d) -> p h d", h=2),
                        tmpz.rearrange("p (h d) -> p h d", h=2), float(np.sqrt(128.0)),
                        gain.unsqueeze(1).to_broadcast([128, 2, 128]), ALU.mult, ALU.mult)
                proj_tm(wb, b, 0, 256, consume)
        v, w = in_view(l, "cbca")
        wb = wload(v, (KC, 8), ("sc_in", l))
        for b in range(NBLK):
            proj_tm(wb, b, 0, 8, lambda pb, b=b: cp("act", bca[:, b, :], pb))
        dump("qnT_l%d_t%d" % (l, it), qnT, [4, TT])
        dump("knT_l%d_t%d" % (l, it), knT, [4, TT])
        dump("cvT_l%d_t%d" % (l, it), cvT, [4, TT])
        Sl = gdnS[l]
        Sb = gdnSb[l]
        I_ = C(C_IDENT)
        bigmod[0] = 3
        for b in range(NBLK):
            bc = slice(b * 128, (b + 1) * 128)
            act(e1, bca[:, b, 0:4], AF.Exp, scale=-1.0)
            act(l1, e1, AF.Ln, bias=1.0)
            act(beta, l1, AF.Exp, scale=-1.0)
            ts("dve", nl1, l1, -1.0, None, ALU.mult)
            tt("dve", t2, bca[:, b, 4:8], sm("dtb")[:, l * 4:(l + 1) * 4], ALU.add)
            act(t2, t2, AF.Exp)
            act(l2, t2, AF.Ln, bias=1.0)
            tt("dve", g_, l2, aexp[:, l, :], ALU.mult)
            pg = bank(7)[:, 0:12]
            mm(pg[:, 0:4], C(C_TRII), g_)
            mm(pg[:, 4:8], C(C_TRIR), g_)
            mm(pg[:, 8:12], C(C_ONES), g_)
            act(eg3, pg.rearrange("p (a h) -> p a h", a=3), AF.Exp)
            ts("dve", ngc, pg[:, 0:4], -1.0, None, ALU.mult)
            tt("dve", bgk, beta, eg3[:, 0, :], ALU.mult)
            cp("dve", Gbc, g_.unsqueeze(2).to_broadcast([128, 4, 128]))
            cp("pool", LBbc, nl1.unsqueeze(2).to_broadcast([128, 4, 128]))
            po = bank(3)
            v4 = lambda ap: ap.rearrange("p (h t) -> p h t", h=4)
            hsl = lambda ap, h: ap[:, h * 128:(h + 1) * 128]
            I4 = I_.unsqueeze(1).to_broadcast([128, 4, 128])
            pgc = bank(4)
            for h in range(4):
                mm(hsl(pgc, h), Gbc[:, h, :], C(C_TRII))
            for h in range(4):
                pe1 = ps[:, 2560 + h * 256:2560 + h * 256 + 128]
                pe0 = ps[:, 2560 + h * 256 + 128:2560 + (h + 1) * 256]
                mm(pe1, Gbc[:, h, :], C(C_TRII), start=True, stop=False)
                mm(pe1, LBbc[:, h, :], I_, start=False, stop=False)
                mm(pe1, I_, C(C_MN1), start=False, stop=True)
                mm(pe0, Gbc[:, h, :], C(C_TRII), start=True, stop=False)
                mm(pe0, I_, C(C_MN0), start=False, stop=True)
            act(EG, v4(pgc), AF.Exp)
            for h in range(4):
                act(EE[:, h, :, :], ps[:, 2560 + h * 256:2560 + (h + 1) * 256].rearrange("p (a t) -> p a t", a=2),
                    AF.Exp, bias=ngc[:, h:h + 1])
            tt("pool", qgT, qnT[:, :, bc], EG, ALU.mult)
            pgk = bank(7)
            pgq = bank(4)
            for h in range(4):
                mm(hsl(pgk, h), knT[:, h, bc], knT[:, h, bc])
            for h in range(4):
                mm(hsl(pgq, h), knT[:, h, bc], qnT[:, h, bc])
            tt("dve", LT, v4(pgk), EE[:, :, 0, :], ALU.mult)
            tt("dve", attnT, v4(pgq), EE[:, :, 1, :], ALU.mult)
            lm = lambda j: C(C_LVL + j - 1).unsqueeze(1).to_broadcast([128, 4, 128])
            tt("pool", Mj[:, 0], LT, lm(1), ALU.mult)
            tt("dve", Ym, I4, Mj[:, 0], ALU.subtract)
            pz = bank(5)
            pyn = bank(6)
            ptn = bank(7)
            for h in range(4):
                tr(hsl(pz, h), Ym[:, h, :], I_)
            cp("act", Tm, v4(pz))
            for j in range(2, 8):
                mj = Mj[:, j % 2]
                tt("pool", mj, LT, lm(j), ALU.mult)
                for h in range(4):
                    mm(hsl(pz, h), mj[:, h, :], Tm[:, h, :])
                tt("dve", Wm, I4, v4(pz), ALU.subtract)
                for h in range(4):
                    mm(hsl(pyn, h), Wm[:, h, :], Ym[:, h, :])
                if j < 7:
                    for h in range(4):
                        mm(hsl(ptn, h), Ym[:, h, :], Wm[:, h, :])
                    cp("dve", Ym, v4(pyn))
                    cp("act", Tm, v4(ptn))
                else:
                    cp("act", Yb, v4(pyn))
            ptr = bank(4).bitcast(BF16)
            for h in range(4):
                tr(ptr[:, h * 128:(h + 1) * 128], cvT[:, h, bc], identb)
                tr(ptr[:, 512 + h * 128:512 + (h + 1) * 128], knT[:, h, bc], identb)
            tt("dve", vb, v4(ptr[:, 0:512]), beta.unsqueeze(2).to_broadcast([128, 4, 128]), ALU.mult)
            tt("dve", kbg, v4(ptr[:, 512:1024]), bgk.unsqueeze(2).to_broadcast([128, 4, 128]), ALU.mult)
            tt("dve", kdec, v4(ptr[:, 512:1024]), eg3[:, 1, :].unsqueeze(2).to_broadcast([128, 4, 128]), ALU.mult)
            pu = bank(5)
            pwt = bank(6)
            for h in range(4):
                mm(hsl(pu, h), Yb[:, h, :], vb[:, h, :])
            for h in range(4):
                mm(hsl(pwt, h), kbg[:, h, :], Yb[:, h, :])
            cp("act", ub, v4(pu))
            cp("dve", wT, v4(pwt))
            pvn = bank(7)
            psn = bank(4)
            for h in range(4):
                mm(hsl(pvn, h), wT[:, h, :], Sb[:, h, :])
            tt("dve", vnew, ub, v4(pvn), ALU.subtract)
            for h in range(4):
                mm(hsl(po, h), qgT[:, h, :], Sb[:, h, :], start=True, stop=False)
                mm(hsl(po, h), attnT[:, h, :], vnew[:, h, :], start=False, stop=True)
            for h in range(4):
                mm(hsl(psn, h), kdec[:, h, :], vnew[:, h, :])
            tt("pool", Sl, Sl, eg3[:, 2, :].unsqueeze(2).to_broadcast([128, 4, 128]), ALU.mult)
            tt("dve", Sl, Sl, v4(psn), ALU.add)
            cp("act", Sb, Sl)
            act(osq, po, AF.Square)
            S.add("dve", lambda e: e.tensor_reduce(ss, osq.rearrange("p (h d) -> p h d", h=4), AX.X, ALU.add),
                  r=[osq], w=[ss])
            act(rstd, ss, AF.Ln, bias=float(128 * EPS))
            act(rstd, rstd, AF.Exp, scale=-0.5)
            tt("dve", og.rearrange("p (h d) -> p h d", h=4), po.rearrange("p (h d) -> p h d", h=4),
               rstd.unsqueeze(2).to_broadcast([128, 4, 128]), ALU.mult)
            tt("pool", mix[:, b, :], og, sgz[:, b, :], ALU.mult)
            mix_to_T(mix, b, 512, 12)
        bigmod[0] = 4
        dump("mixC_l%d_t%d" % (l, it), mix, [NBLK, 512])

    def out_proj(l, it):
        mixT = A.at(R + MIXT_OFF, [KC, TT], BF16)
        for blk in range(8):
            wb = wload(sc_out[l][blk], (KC, 256), ("sc_out", l))
            for c in range(2):
                m = blk * 2 + c
                pb = bigbank()
                for kc in range(KC):
                    mm(pb, wb[:, kc, c * 128:(c + 1) * 128], mixT[:, kc, :], start=(kc == 0), stop=(kc == KC - 1))
                stt("dve", xT[:, m, :], pb, modT[:, l, 2, m:m + 1], xT[:, m, :], ALU.mult, ALU.add)

    def ffn(l, it):
        aT = A.at(R, [FC, TT], BF16)
        ftmp = A.at(R + 45056, [2, TT], F32)
        fm_norm(l, 1, hT, R + 49152)
        for f2 in range(22):
            wgb = wload(sc_g[l][f2], (KC, 256), ("sc_g", l))
            wub = wload(sc_u[l][f2], (KC, 256), ("sc_u", l))
            if f2 == 0:
                bks = [bigbank() for _ in range(4)]
                for kc in range(KC):
                    for c in range(2):
                        mm(bks[2 * c], wgb[:, kc, c * 128:(c + 1) * 128], hT[:, kc, :], start=(kc == 0), stop=(kc == KC - 1))
                        mm(bks[2 * c + 1], wub[:, kc, c * 128:(c + 1) * 128], hT[:, kc, :], start=(kc == 0), stop=(kc == KC - 1))
                for c in range(2):
                    ft = ftmp[:, c % 2, :]
                    act(ft, bks[2 * c], AF.Silu)
                    tt("dve", aT[:, c, :], ft, bks[2 * c + 1], ALU.mult)
                continue
            for c in range(2):
                f = f2 * 2 + c
                pg_ = bigbank()
                for kc in range(KC):
                    mm(pg_, wgb[:, kc, c * 128:(c + 1) * 128], hT[:, kc, :], start=(kc == 0), stop=(kc == KC - 1))
                pu_ = bigbank()
                for kc in range(KC):
                    mm(pu_, wub[:, kc, c * 128:(c + 1) * 128], hT[:, kc, :], start=(kc == 0), stop=(kc == KC - 1))
                ft = ftmp[:, f % 2, :]
                act(ft, pg_, AF.Silu)
                tt("dve", aT[:, f, :], ft, pu_, ALU.mult)
        for m in range(16):
            pb = bigbank()
            for kh in range(2):
                wb = wload(sc_d[l][m * 2 + kh], (22, 128), ("sc_d", l))
                for k in range(22):
                    mm(pb, wb[:, k, :], aT[:, kh * 22 + k, :], start=(kh == 0 and k == 0), stop=(kh == 1 and k == 21))
            stt("dve", xT[:, m, :], pb, modT[:, l, 5, m:m + 1], xT[:, m, :], ALU.mult, ALU.add)

    def tile_tail(it):
        t0 = it * TT
        fm_norm(None, 0, xT, R + 49152)
        for b in range(NBLK):
            ostage = A.at(R + (b % 2) * 8192, [D], F32)
            for k4 in range(4):
                pb = bigbank()
                for j in range(4):
                    tr(pb[:, j * 128:(j + 1) * 128], xT[:, k4 * 4 + j, b * 128:(b + 1) * 128], C(C_IDENT))
                cp("dve" if k4 % 2 == 0 else "act", ostage[:, k4 * 512:(k4 + 1) * 512], pb)
            out_ops.append(dma("sp", out_d[t0 + b * 128:t0 + (b + 1) * 128, :], ostage, g_out[b % 2], r=[ostage]))

    def finish():
        if SCHEDULE:
            S.schedule()
        S.emit(out_ops + dump_ops)
        return nc

    stop_after = stop_after or "none"
    for it in range(NT):
        tile_head(it)
        for l in range(L):
            ec = (lambda i: emit_casts(l + 1, i, 6)) if it == 0 else (lambda i: None)
            ec(0)
            fm_norm(l, 0, hT, R + MB_OFF)
            dump("hT_l%d_t%d" % (l, it), hT, [KC, TT])
            if stop_after == "norm":
                return finish()
            ec(1)
            swa(l, it)
            if stop_after.startswith("swa"):
                return finish()
            ec(2)
            gla(l, it)
            if stop_after == "gla":
                return finish()
            ec(3)
            gdn(l, it)
            if stop_after == "gdn":
                return finish()
            ec(4)
            out_proj(l, it)
            dump("xTmid_l%d_t%d" % (l, it), xT, [KC, TT])
            ec(5)
            ffn(l, it)
            dump("xTend_l%d_t%d" % (l, it), xT, [KC, TT])
        tile_tail(it)
    return finish()


_NC_CACHE = {}


def kernel(x, c, positions, w_mod, b_mod, norm1_gain, norm2_gain, w_in, gla_w_gk, gla_b_gk,
           gla_norm_gain, swa_sinks, gdn_conv_w, gdn_a_log, gdn_dt_bias, gdn_norm_gain, w_out,
           ffn_w_gate, ffn_w_up, ffn_w_down, final_norm_gain):
    inp = dict(x=x, c=c, positions=positions, w_mod=w_mod, b_mod=b_mod, norm1_gain=norm1_gain,
               norm2_gain=norm2_gain, w_in=w_in, gla_w_gk=gla_w_gk, gla_b_gk=gla_b_gk,
               gla_norm_gain=gla_norm_gain, swa_sinks=swa_sinks, gdn_conv_w=gdn_conv_w,
               gdn_a_log=gdn_a_log, gdn_dt_bias=gdn_dt_bias, gdn_norm_gain=gdn_norm_gain, w_out=w_out,
               ffn_w_gate=ffn_w_gate, ffn_w_up=ffn_w_up, ffn_w_down=ffn_w_down,
               final_norm_gain=final_norm_gain)
    inp = {k: np.asarray(v) for k, v in inp.items()}
    B, T, _ = inp["x"].shape
    L = inp["w_in"].shape[0]
    nc = build(T, L)
    consts, _ = make_consts()
    f32c = lambda a: np.ascontiguousarray(a, dtype=np.float32)
    shared = {"consts": consts, "w_mod": f32c(inp["w_mod"]), "w_in": f32c(inp["w_in"]), "w_out": f32c(inp["w_out"]),
              "w_gate": f32c(inp["ffn_w_gate"]), "w_up": f32c(inp["ffn_w_up"]), "w_down": f32c(inp["ffn_w_down"])}
    in_maps = []
    for b in range(B):
        m = dict(shared)
        m["x"] = f32c(inp["x"][b])
        m["pos"] = np.ascontiguousarray(np.broadcast_to(inp["positions"][b][None, :], (128, T))).astype(np.int32)
        m["small"] = pack_small(inp, b, L)
        in_maps.append(m)
    res = run_bass_kernel_spmd(nc, in_maps, core_ids=list(range(B)))
    return np.stack([np.asarray(r["out"], dtype=np.float32) for r in res.results], axis=0)
```
